# Optimizing a Trainium2 kernel written in Bass

```python
import math
import jax, jax.numpy as jnp
from jax import lax
import numpy as np

D_MODEL = 1024
BATCH = 8
SEQ = 2048
DEPTH = 2

FFN_DIM = 2816
RG_WIDTH = D_MODEL
RG_BLOCKS = 16
RG_BLOCK_DIM = RG_WIDTH // RG_BLOCKS
RG_C = 8.0
CONV_WIDTH = 4
ATT_GROUPS = ((128, 1), (512, 4), (2048, 16))
ATT_HEADS_PER_GROUP = 4
ATT_HEADS = ATT_HEADS_PER_GROUP * len(ATT_GROUPS)
ATT_HEAD_DIM = 64
ATT_WIDTH = ATT_HEADS * ATT_HEAD_DIM
DN_HEADS = 8
DN_HEAD_DIM = 128
DN_WIDTH = DN_HEADS * DN_HEAD_DIM
DN_CHUNK = 64
N_BRANCH = 3
EPS = 1e-6
NEG_INF = -1e30

IN_SPLITS = (RG_WIDTH, RG_WIDTH, ATT_WIDTH, ATT_WIDTH, ATT_WIDTH,
             DN_WIDTH, DN_WIDTH, DN_WIDTH, DN_WIDTH, DN_HEADS, DN_HEADS, N_BRANCH * D_MODEL)
IN_DIM = sum(IN_SPLITS)
IN_OFFSETS = tuple(int(v) for v in np.cumsum(IN_SPLITS)[:-1])
BRANCH_DIM = RG_WIDTH + ATT_WIDTH + DN_WIDTH
BRANCH_OFFSETS = (RG_WIDTH, RG_WIDTH + ATT_WIDTH)

kernel_name = 'hybrid_rglru_dilated_attn_gated_deltanet_macaron'


def rms_norm(x, g):
    xf = x.astype(jnp.float32)
    y = xf * lax.rsqrt(jnp.mean(xf * xf, axis=-1, keepdims=True) + EPS)
    return (y * g.astype(jnp.float32)).astype(x.dtype)


def l2_norm(x):
    xf = x.astype(jnp.float32)
    return xf * lax.rsqrt(jnp.sum(xf * xf, axis=-1, keepdims=True) + EPS)


def swiglu(x, w_gate, w_up, w_down):
    return (jax.nn.silu(x @ w_gate) * (x @ w_up)) @ w_down


def causal_depthwise_conv(x, w):
    K = w.shape[0]
    T = x.shape[1]
    xp = jnp.pad(x, ((0, 0), (K - 1, 0), (0, 0)))
    out = xp[:, 0:T] * w[0]
    for k in range(1, K):
        out = out + xp[:, k:k + T] * w[k]
    return out


def rg_lru(x, w_r, b_r, w_i, b_i, lam):
    B, T, C = x.shape
    xb = x.reshape(B, T, RG_BLOCKS, RG_BLOCK_DIM)
    r = jax.nn.sigmoid((jnp.einsum('btnc,ncd->btnd', xb, w_r).reshape(B, T, C) + b_r).astype(jnp.float32))
    i = jax.nn.sigmoid((jnp.einsum('btnc,ncd->btnd', xb, w_i).reshape(B, T, C) + b_i).astype(jnp.float32))
    log_a = -RG_C * r * jax.nn.softplus(-lam.astype(jnp.float32))
    a = jnp.exp(log_a)
    b = jnp.sqrt(-jnp.expm1(2.0 * log_a)) * (i * x.astype(jnp.float32))

    def combine(c1, c2):
        a1, b1 = c1
        a2, b2 = c2
        return a1 * a2, a2 * b1 + b2

    _, h = lax.associative_scan(combine, (a, b), axis=1)
    return h.astype(x.dtype)


def alibi_slopes():
    h = jnp.arange(1, ATT_HEADS + 1, dtype=jnp.float32)
    return jnp.exp2(-8.0 * h / ATT_HEADS)


def dilated_window_attention(q, k, v, slopes, window, dilation):
    B, T, H, hd = q.shape
    span = window // dilation
    L = T // dilation
    nblk = -(-L // span)
    Lp = nblk * span

    def to_blocks(t):
        t = t.reshape(B, L, dilation, H, hd).transpose(0, 2, 1, 3, 4)
        t = jnp.pad(t, ((0, 0), (0, 0), (0, Lp - L), (0, 0), (0, 0)))
        return t.reshape(B, dilation, nblk, span, H, hd)

    def with_prev(t):
        prev = jnp.pad(t, ((0, 0), (0, 0), (1, 0), (0, 0), (0, 0), (0, 0)))[:, :, :-1]
        return jnp.concatenate([prev, t], axis=3)

    qb = to_blocks(q)
    kw = with_prev(to_blocks(k))
    vw = with_prev(to_blocks(v))
    s = jnp.einsum('brnqhe,brnkhe->brnhqk', qb, kw).astype(jnp.float32)
    qi = jnp.arange(span)[:, None]
    kj = jnp.arange(2 * span)[None, :]
    delta = qi + span - kj
    blk = jnp.arange(nblk)[:, None, None]
    valid = (delta >= 0) & (delta <= span) & ((blk > 0) | (kj >= span))
    bias = -slopes[:, None, None] * (delta * dilation).astype(jnp.float32)
    s = jnp.where(valid[:, None], s + bias, NEG_INF)
    m = jnp.max(s, axis=-1, keepdims=True)
    p = jnp.exp(s - m)
    den = jnp.sum(p, axis=-1, keepdims=True)
    o = jnp.einsum('brnhqk,brnkhe->brnqhe', (p / den).astype(v.dtype), vw)
    lse = (m + jnp.log(den))[..., 0]
    o = o.reshape(B, dilation, Lp, H, hd)[:, :, :L].transpose(0, 2, 1, 3, 4).reshape(B, T, H, hd)
    lse = lse.transpose(0, 1, 2, 4, 3).reshape(B, dilation, Lp, H)[:, :, :L]
    lse = lse.transpose(0, 2, 1, 3).reshape(B, T, H)
    return o, lse


def chunk_gated_delta_rule(q, k, v, g_log, beta):
    B, T, H, dk = q.shape
    dv = v.shape[-1]
    C = DN_CHUNK
    N = T // C
    f32 = jnp.float32

    def chunks(t):
        t = t.astype(f32).reshape((B, N, C, H) + t.shape[3:])
        return jnp.moveaxis(t, 3, 1)

    qc, kc, vc = chunks(q), chunks(k), chunks(v)
    gc, bc = chunks(g_log), chunks(beta)
    gam = jnp.cumsum(gc, axis=-1)
    incl = jnp.tril(jnp.ones((C, C), bool))
    strict = jnp.tril(jnp.ones((C, C), bool), -1)
    diff = gam[..., :, None] - gam[..., None, :]
    decay = jnp.where(incl, jnp.exp(jnp.where(incl, diff, 0.0)), 0.0)
    kb = kc * bc[..., None]
    vb = vc * bc[..., None]
    a = jnp.where(strict, jnp.einsum('bhnid,bhnjd->bhnij', kb, kc) * decay, 0.0)
    rhs = jnp.concatenate([vb, kb * jnp.exp(gam)[..., None]], axis=-1)
    sol = lax.linalg.triangular_solve(a + jnp.eye(C, dtype=f32), rhs, left_side=True, lower=True,
                                      unit_diagonal=True)
    u0, wk = sol[..., :dv], sol[..., dv:]
    qk = jnp.where(incl, jnp.einsum('bhnid,bhnjd->bhnij', qc, kc) * decay, 0.0)
    q_dec = qc * jnp.exp(gam)[..., None]
    k_dec = kc * jnp.exp(gam[..., -1:] - gam)[..., None]
    c_dec = jnp.exp(gam[..., -1])
    xs = tuple(jnp.moveaxis(t, 2, 0) for t in (u0, wk, qk, q_dec, k_dec, c_dec))

    def step(S, inp):
        u0_n, w_n, qk_n, qd_n, kd_n, cd_n = inp
        u = u0_n - jnp.einsum('bhck,bhkv->bhcv', w_n, S)
        o = jnp.einsum('bhck,bhkv->bhcv', qd_n, S) + jnp.einsum('bhij,bhjv->bhiv', qk_n, u)
        S = S * cd_n[..., None, None] + jnp.einsum('bhck,bhcv->bhkv', kd_n, u)
        return S, o

    S0 = jnp.zeros((B, H, dk, dv), f32)
    _, o = lax.scan(step, S0, xs)
    return jnp.transpose(o, (1, 0, 3, 2, 4)).reshape(B, T, H, dv)


def hybrid_mixer(u, w_in, rg_conv_w, rg_conv_b, rg_w_r, rg_b_r, rg_w_i, rg_b_i, rg_lambda,
                 att_q_norm, att_k_norm, dn_conv_w, dn_a_log, dn_dt_bias, dn_out_norm, w_branch, w_out):
    B, T, _ = u.shape
    proj = u @ w_in
    (rg_x, rg_gate, aq, ak, av, dq, dk_, dv_, dz, d_beta, d_alpha, merge_logits) = jnp.split(
        proj, IN_OFFSETS, axis=-1)

    xa = causal_depthwise_conv(rg_x, rg_conv_w) + rg_conv_b
    ya = rg_lru(xa, rg_w_r, rg_b_r, rg_w_i, rg_b_i, rg_lambda) * jax.nn.gelu(rg_gate)

    hshape = (B, T, ATT_HEADS, ATT_HEAD_DIM)
    q = rms_norm(aq.reshape(hshape), att_q_norm) * (ATT_HEAD_DIM ** -0.5)
    k = rms_norm(ak.reshape(hshape), att_k_norm)
    v = av.reshape(hshape)
    slopes = alibi_slopes()
    outs, lses = [], []
    for g, (window, dilation) in enumerate(ATT_GROUPS):
        hs = slice(g * ATT_HEADS_PER_GROUP, (g + 1) * ATT_HEADS_PER_GROUP)
        o_g, lse_g = dilated_window_attention(q[:, :, hs], k[:, :, hs], v[:, :, hs], slopes[hs], window, dilation)
        outs.append(o_g)
        lses.append(lse_g)
    wts = jax.nn.softmax(jnp.stack(lses, axis=0), axis=0)
    o_att = jnp.stack(outs, axis=0) * wts[..., None].astype(v.dtype)
    yb = o_att.transpose(1, 2, 0, 3, 4).reshape(B, T, ATT_WIDTH)

    qkv = jax.nn.silu(causal_depthwise_conv(jnp.concatenate([dq, dk_, dv_], axis=-1), dn_conv_w))
    cq, ck, cv = jnp.split(qkv, 3, axis=-1)
    dshape = (B, T, DN_HEADS, DN_HEAD_DIM)
    qd = l2_norm(cq.reshape(dshape)) * (DN_HEAD_DIM ** -0.5)
    kd = l2_norm(ck.reshape(dshape))
    vd = cv.reshape(dshape)
    beta = jax.nn.sigmoid(d_beta.astype(jnp.float32))
    g_log = -jnp.exp(dn_a_log.astype(jnp.float32)) * jax.nn.softplus(
        d_alpha.astype(jnp.float32) + dn_dt_bias.astype(jnp.float32))
    o_dn = chunk_gated_delta_rule(qd, kd, vd, g_log, beta)
    yc = (rms_norm(o_dn, dn_out_norm) * jax.nn.silu(dz.reshape(dshape).astype(jnp.float32)))
    yc = yc.reshape(B, T, DN_WIDTH).astype(u.dtype)

    gates = jax.nn.sigmoid(merge_logits.reshape(B, T, N_BRANCH, D_MODEL))
    wa, wb, wc = jnp.split(w_branch, BRANCH_OFFSETS, axis=0)
    y = gates[:, :, 0] * (ya @ wa) + gates[:, :, 1] * (yb @ wb) + gates[:, :, 2] * (yc @ wc)
    return y @ w_out


def setup_inputs(seed: int = 0) -> dict:
    key = jax.random.key(seed)
    ks = jax.random.split(key, 32)
    f32 = jnp.float32
    L, D, F = DEPTH, D_MODEL, FFN_DIM

    def dense(k, shape, fan_in):
        return jax.random.normal(k, shape, f32) * (fan_in ** -0.5)

    def gain(k, shape):
        return 1.0 + 0.05 * jax.random.normal(k, shape, f32)

    def bias(k, shape):
        return 0.02 * jax.random.normal(k, shape, f32)

    x = jax.random.normal(ks[0], (BATCH, SEQ, D), f32)
    a0 = jax.random.uniform(ks[13], (L, RG_WIDTH), f32, 0.9, 0.999)
    s = a0 ** (1.0 / RG_C)
    rg_lambda = jnp.log(s) - jnp.log1p(-s)
    dn_a_log = jnp.log(jax.random.uniform(ks[17], (L, DN_HEADS), f32, 1.0, 16.0))
    dt = jnp.exp(jax.random.uniform(ks[18], (L, DN_HEADS), f32, math.log(1e-3), math.log(1e-1)))
    dn_dt_bias = dt + jnp.log(-jnp.expm1(-dt))
    w_branch = jnp.concatenate([
        dense(ks[20], (L, RG_WIDTH, D), RG_WIDTH),
        dense(ks[21], (L, ATT_WIDTH, D), ATT_WIDTH),
        dense(ks[22], (L, DN_WIDTH, D), DN_WIDTH)], axis=1)
    return {
        'x': x,
        'ffn1_norm': gain(ks[1], (L, D)),
        'ffn1_w_gate': dense(ks[2], (L, D, F), D),
        'ffn1_w_up': dense(ks[3], (L, D, F), D),
        'ffn1_w_down': dense(ks[4], (L, F, D), F),
        'mix_norm': gain(ks[5], (L, D)),
        'w_in': dense(ks[6], (L, D, IN_DIM), D),
        'rg_conv_w': dense(ks[7], (L, CONV_WIDTH, RG_WIDTH), CONV_WIDTH),
        'rg_conv_b': bias(ks[8], (L, RG_WIDTH)),
        'rg_w_r': dense(ks[9], (L, RG_BLOCKS, RG_BLOCK_DIM, RG_BLOCK_DIM), RG_BLOCK_DIM),
        'rg_b_r': bias(ks[10], (L, RG_WIDTH)),
        'rg_w_i': dense(ks[11], (L, RG_BLOCKS, RG_BLOCK_DIM, RG_BLOCK_DIM), RG_BLOCK_DIM),
        'rg_b_i': bias(ks[12], (L, RG_WIDTH)),
        'rg_lambda': rg_lambda,
        'att_q_norm': gain(ks[14], (L, ATT_HEADS, ATT_HEAD_DIM)),
        'att_k_norm': gain(ks[15], (L, ATT_HEADS, ATT_HEAD_DIM)),
        'dn_conv_w': dense(ks[16], (L, CONV_WIDTH, 3 * DN_WIDTH), CONV_WIDTH),
        'dn_a_log': dn_a_log,
        'dn_dt_bias': dn_dt_bias,
        'dn_out_norm': gain(ks[19], (L, DN_HEADS, DN_HEAD_DIM)),
        'w_branch': w_branch,
        'w_out': dense(ks[23], (L, D, D), D),
        'ffn2_norm': gain(ks[24], (L, D)),
        'ffn2_w_gate': dense(ks[25], (L, D, F), D),
        'ffn2_w_up': dense(ks[26], (L, D, F), D),
        'ffn2_w_down': dense(ks[27], (L, F, D), F),
    }


def reference(x, ffn1_norm, ffn1_w_gate, ffn1_w_up, ffn1_w_down, mix_norm, w_in,
              rg_conv_w, rg_conv_b, rg_w_r, rg_b_r, rg_w_i, rg_b_i, rg_lambda,
              att_q_norm, att_k_norm, dn_conv_w, dn_a_log, dn_dt_bias, dn_out_norm,
              w_branch, w_out, ffn2_norm, ffn2_w_gate, ffn2_w_up, ffn2_w_down):
    for l in range(DEPTH):
        x = x + 0.5 * swiglu(rms_norm(x, ffn1_norm[l]), ffn1_w_gate[l], ffn1_w_up[l], ffn1_w_down[l])
        x = x + hybrid_mixer(rms_norm(x, mix_norm[l]), w_in[l], rg_conv_w[l], rg_conv_b[l],
                             rg_w_r[l], rg_b_r[l], rg_w_i[l], rg_b_i[l], rg_lambda[l],
                             att_q_norm[l], att_k_norm[l], dn_conv_w[l], dn_a_log[l], dn_dt_bias[l],
                             dn_out_norm[l], w_branch[l], w_out[l])
        x = x + 0.5 * swiglu(rms_norm(x, ffn2_norm[l]), ffn2_w_gate[l], ffn2_w_up[l], ffn2_w_down[l])
    return x
```

```python
import contextlib
import os as _os
import numpy as np
import concourse.bass as bass
import concourse.mybir as mybir
from concourse.bass_utils import run_bass_kernel_spmd

F32 = mybir.dt.float32
BF16 = mybir.dt.bfloat16
AF = mybir.ActivationFunctionType
ALU = mybir.AluOpType
ESZ = {F32: 4, BF16: 2}

T = 2048
D = 1024
FF = 2816
L = 2
KC = 8
IN_DIM = 11536
O_RGX, O_RGG, O_AQ, O_AK, O_AV = 0, 1024, 2048, 2816, 3584
O_DQ, O_DK, O_DV, O_DZ, O_BETA, O_ALPHA, O_MG = 4352, 5376, 6400, 7424, 8448, 8456, 8464
EPS = 1e-6
NEG = -30000.0
ATT_GROUPS = ((128, 1), (512, 4), (2048, 16))

P_N1, P_NM, P_N2, P_RCW, P_RCB, P_RBR, P_RBI, P_LAM = 0, 8, 16, 24, 56, 64, 72, 80
P_AQN, P_AKN, P_DCW, P_DON, P_ALOG, P_DTB = 88, 94, 100, 196, 204, 212
NPV = 224
C_ID, C_NEGUT, C_NEGLT, C_BD, C_OFF, C_UT, C_ONES, C_AB = 0, 128, 256, 384, 512, 640, 768, 896
NCF = 896 + 3072
A_XT, A_UT, A_CF, A_CB, A_PV, A_DER, A_WORK, A_END = 0, 16384, 24576, 28544, 28672, 29120, 29184, 51200


def _box(ap):
    es = ESZ[ap.dtype]
    pat = ap.ap
    space = str(ap.space)
    if space == "DRAM":
        return ("DRAM", ap.tensor.name, 0, 1, 0, 1)
    pstep, pcnt = pat[0]
    if pstep == 0:
        row = int(np.prod(ap.tensor.shape[1:]))
        pcnt = 1
    else:
        row = pstep
    p0 = ap.offset // row
    c0 = ap.offset % row
    ext = 1
    for st, cnt in pat[1:]:
        ext += (cnt - 1) * abs(st)
    return (space, ap.tensor.name, p0, p0 + pcnt, c0 * es, (c0 + ext) * es)


class Sched:
    ENGS = ("pe", "act", "dve", "pool", "sp")
    BUCKET = 256
    EPOCH = 4000
    NDMA = 24

    def __init__(self, nc):
        self.nc = nc
        self.ops = {e: [] for e in self.ENGS}
        self.tab = {}
        self.ndma = {e: 0 for e in self.ENGS}

    def _keys(self, box):
        space, name, p0, p1, b0, b1 = box
        if space == "DRAM":
            return [(space, name, 0, 0)]
        if space == "PSUM":
            return [(space, name, 0, b) for b in range(b0 // 2048, (b1 - 1) // 2048 + 1)]
        ks = []
        for q in range(p0 // 32, (p1 - 1) // 32 + 1):
            for b in range(b0 // self.BUCKET, (b1 - 1) // self.BUCKET + 1):
                ks.append((space, name, q, b))
        return ks

    def op(self, eng, fn, reads, writes, dma=False):
        idx = len(self.ops[eng])
        if dma:
            tok = ("dma", eng, self.ndma[eng])
            self.ndma[eng] += 1
        else:
            tok = (eng, idx)
        deps = set()
        rk, wk = [], []
        for ap in reads:
            rk += self._keys(_box(ap))
        for ap in writes:
            wk += self._keys(_box(ap))
        for k in rk:
            ent = self.tab.get(k)
            if ent is not None and ent[0] is not None:
                deps.add(ent[0])
        for k in wk:
            ent = self.tab.get(k)
            if ent is not None:
                if ent[0] is not None:
                    deps.add(ent[0])
                deps.update(ent[1].values())
        for k in rk:
            ent = self.tab.setdefault(k, [None, {}])
            ent[1][tok if dma else tok[:-1]] = tok
        for k in wk:
            self.tab[k] = [tok, {}]
        deps.discard(tok)
        self.ops[eng].append({"fn": fn, "deps": deps, "dma": tok if dma else None})
        return tok

    def mm(self, out, lhsT, rhs, start=True, stop=True):
        return self.op("pe", lambda e: e.matmul(out, lhsT, rhs, start=start, stop=stop),
                       [lhsT, rhs] + ([] if start else [out]), [out])

    def tr(self, out, in_, ident):
        return self.op("pe", lambda e: e.transpose(out, in_, ident), [in_, ident], [out])

    def act(self, out, in_, func, bias=None, scale=None):
        kw = {}
        rd = [in_]
        if bias is not None:
            kw["bias"] = bias
            if not isinstance(bias, (int, float)):
                rd.append(bias)
        if scale is not None:
            kw["scale"] = scale
            if not isinstance(scale, (int, float)):
                rd.append(scale)
        return self.op("act", lambda e: e.activation(out, in_, func, **kw), rd, [out])

    def tt(self, eng, out, in0, in1, op):
        return self.op(eng, lambda e: e.tensor_tensor(out, in0, in1, op), [in0, in1], [out])

    def ts(self, eng, out, in0, s1, s2, op0, op1=None):
        rd = [in0] + [s for s in (s1, s2) if s is not None and not isinstance(s, (int, float))]
        if op1 is None:
            return self.op(eng, lambda e: e.tensor_scalar(out, in0, s1, None, op0), rd, [out])
        return self.op(eng, lambda e: e.tensor_scalar(out, in0, s1, s2, op0, op1), rd, [out])

    def stt(self, eng, out, in0, scalar, in1, op0, op1):
        rd = [in0, in1] + ([] if isinstance(scalar, (int, float)) else [scalar])
        return self.op(eng, lambda e: e.scalar_tensor_tensor(out, in0, scalar, in1, op0, op1), rd, [out])

    def copy(self, eng, out, in_):
        if eng == "act":
            return self.op(eng, lambda e: e.copy(out, in_), [in_], [out])
        return self.op(eng, lambda e: e.tensor_copy(out, in_), [in_], [out])

    def scan(self, out, d0, d1, init, op0, op1):
        rd = [d0, d1] + ([] if isinstance(init, (int, float)) else [init])
        return self.op("dve", lambda e: e.tensor_tensor_scan(out, d0, d1, init, op0, op1), rd, [out])

    def recip(self, out, in_):
        return self.op("dve", lambda e: e.reciprocal(out, in_), [in_], [out])

    def memset(self, eng, ap, val):
        return self.op(eng, lambda e: e.memset(ap, val), [], [ap])

    def dma(self, q, out, in_):
        return self.op(q, lambda e: e.dma_start(out=out, in_=in_), [in_], [out], dma=True)

    def emit(self):
        nc = self.nc
        engs = self.ENGS
        sig = {e: set() for e in engs}
        for f in engs:
            for rec in self.ops[f]:
                for d in rec["deps"]:
                    if d[0] == "dma":
                        continue
                    e, i = d
                    if e == f and e == "pe":
                        continue
                    sig[e].add(i)
        cnt = {e: {} for e in engs}
        for e in engs:
            for c, i in enumerate(sorted(sig[e])):
                cnt[e][i] = c + 1
        nep = {e: (len(cnt[e]) + self.EPOCH - 1) // self.EPOCH for e in engs}
        for e in engs:
            for i, rec in enumerate(self.ops[e]):
                rec["idx"] = i
        with contextlib.ExitStack() as st:
            esem = {e: [st.enter_context(nc.semaphore(f"s_{e}_{k}")) for k in range(nep[e])] for e in engs}
            dsem = {e: [st.enter_context(nc.semaphore(f"d_{e}_{k}")) for k in range(min(self.NDMA, self.ndma[e]))]
                    for e in engs}
            block = st.enter_context(nc.Block())

            def gen(f):
                def body(eobj):
                    waited = {}
                    for rec in self.ops[f]:
                        need = {}
                        for d in rec["deps"]:
                            if d[0] == "dma":
                                _, q, k = d
                                key = ("d", q, k % self.NDMA)
                                val = 16 * (k // self.NDMA + 1)
                            else:
                                e, i = d
                                if e == f and e == "pe":
                                    continue
                                c = cnt[e][i]
                                key = ("e", e, (c - 1) // self.EPOCH)
                                val = (c - 1) % self.EPOCH + 1
                            if need.get(key, 0) < val:
                                need[key] = val
                        if rec["dma"] is not None:
                            _, q, k = rec["dma"]
                            if k >= self.NDMA:
                                key = ("d", q, k % self.NDMA)
                                val = 16 * (k // self.NDMA)
                                if need.get(key, 0) < val:
                                    need[key] = val
                        for key, val in need.items():
                            if waited.get(key, 0) >= val:
                                continue
                            waited[key] = val
                            sem = dsem[key[1]][key[2]] if key[0] == "d" else esem[key[1]][key[2]]
                            eobj.wait_ge(sem, val)
                        ins = rec["fn"](eobj)
                        if rec["dma"] is not None:
                            _, q, k = rec["dma"]
                            ins.then_inc(dsem[q][k % self.NDMA], 16)
                        elif rec["idx"] in cnt[f]:
                            c = cnt[f][rec["idx"]]
                            ins.then_inc(esem[f][(c - 1) // self.EPOCH], 1)
                    if f == "sp":
                        for q in engs:
                            for s in range(min(self.NDMA, self.ndma[q])):
                                n = (self.ndma[q] - 1 - s) // self.NDMA + 1
                                eobj.wait_ge(dsem[q][s], 16 * n)
                return body

            block.tensor(gen("pe"))
            block.scalar(gen("act"))
            block.vector(gen("dve"))
            block.gpsimd(gen("pool"))
            block.sync(gen("sp"))


def interleave(gens):
    gens = list(gens)
    while gens:
        for g in list(gens):
            try:
                next(g)
            except StopIteration:
                gens.remove(g)


class Builder:
    def __init__(self, nlayers=L, stop_after=None):
        self.nlayers = nlayers
        self.stop_after = stop_after
        nc = bass.Bass("TRN2", target_bir_lowering=False)
        self.nc = nc
        dt = lambda name, shape, kind="ExternalInput": nc.dram_tensor(name, shape, F32, kind=kind).ap()
        self.xin = dt("xT", [D, T])
        self.cf = dt("cf", [128, NCF])
        self.cb = dt("cb", [128, 256])
        self.pv = dt("pv", [L, 128, NPV])
        self.rgw = dt("rgw", [L, 128, 2048])
        self.w = {}
        for f in ("ffn1", "ffn2"):
            self.w[f + "_g"] = dt(f + "_w_gate", [L, D, FF])
            self.w[f + "_u"] = dt(f + "_w_up", [L, D, FF])
            self.w[f + "_d"] = dt(f + "_w_down", [L, FF, D])
        self.w_in = dt("w_in", [L, D, IN_DIM])
        self.w_br = dt("w_branch", [L, FF, D])
        self.w_out = dt("w_out", [L, D, D])
        self.xsp = dt("xspill", [D, T], kind="Internal")
        self.yout = dt("y", [D, T], kind="ExternalOutput")
        self.psb = 0
        self.dbg_names = []
        self.dbgs = set()

    def fv(self, c0, n):
        return self.ar[:, c0:c0 + n]

    def bv(self, c0, nbf):
        return self.ar[:, c0:c0 + nbf // 2].bitcast(BF16)

    def bank(self):
        b = self.psb
        self.psb = (self.psb + 1) % 8
        return self.ps[:, b * 512:(b + 1) * 512]

    def build(self):
        nc = self.nc
        with (nc.sbuf_tensor("arena", [128, A_END], F32) as ar, nc.psum_tensor("psum", [128, 4096], F32) as ps):
            self.ar = ar
            self.ps = ps
            self.S = Sched(nc)
            self.program()
            self.S.emit()
        return nc

    def program(self):
        S = self.S
        self.xT = self.fv(A_XT, 16384).rearrange("p (k t) -> p k t", t=T)
        self.uT = self.bv(A_UT, 16384).rearrange("p (k t) -> p k t", t=T)
        self.CF = self.fv(A_CF, NCF)
        self.CB = self.bv(A_CB, 256)
        self.ident = self.CF[:, C_ID:C_ID + 128]
        self.ones_b = self.CB[:, 0:128]
        self.bd64_b = self.CB[:, 128:256]
        S.dma("sp", self.CF, self.cf[:, :])
        S.dma("pool", self.CB, self.cb[:, :])
        for l in range(L):
            S.dma("sp", self.fv(A_PV + l * NPV, NPV), self.pv[l, :, :])
        for k in range(KC):
            S.dma("sp", self.xT[:, k, :], self.xin[k * 128:(k + 1) * 128, :])
        done = False
        for l in range(self.nlayers):
            self.PV = self.fv(A_PV + l * NPV, NPV)
            if not (self.stop_after and self.stop_after[1].endswith("_only")):
                self.ffn(l, "ffn1", P_N1)
            if self.stop_after == (l, "ffn1"):
                done = True
                break
            self.mixer(l)
            if self.stop_after is not None and self.stop_after[0] == l and self.stop_after[1].startswith("mix"):
                done = True
                break
            last = (l == self.nlayers - 1)
            self.ffn(l, "ffn2", P_N2, store=last)
            if last:
                return
        for k in range(KC):
            S.dma("sp", self.yout[k * 128:(k + 1) * 128, :], self.xT[:, k, :])

    def rmsnorm(self, pcol, wbase):
        S = self.S
        for t in range(4):
            sl = slice(t * 512, (t + 1) * 512)
            ps = self.bank()
            R = self.fv(wbase + 1024 + (t % 2) * 512, 512)
            for k in range(KC):
                sq = self.bv(wbase + (k % 4) * 256, 512)
                S.act(sq, self.xT[:, k, sl], AF.Square)
                S.mm(ps, self.ones_b, sq, start=(k == 0), stop=(k == KC - 1))
            S.act(R, ps, AF.Ln, bias=EPS, scale=1.0 / D)
            S.act(R, R, AF.Exp, scale=-0.5)
            for k in range(KC):
                S.stt("dve", self.uT[:, k, sl], self.xT[:, k, sl], self.PV[:, pcol + k:pcol + k + 1], R,
                      ALU.mult, ALU.mult)

    def wload(self, dst, src2d, nk):
        self.S.dma("pool", dst, src2d.rearrange("(k p) n -> p k n", p=128))

    def ffn(self, l, name, pcol, store=False):
        S = self.S
        W = A_WORK
        hT = self.bv(W, 22 * 1024).rearrange("p (f t) -> p f t", t=1024)
        wgu = [self.bv(W + 11264 + i * 1024, 2048).rearrange("p (k n) -> p k n", n=256) for i in range(4)]
        wd = [self.bv(W + 15360 + i * 1408, 2816).rearrange("p (f n) -> p f n", n=128) for i in range(2)]
        sg = [self.fv(W + 18176 + i * 512, 512) for i in range(2)]
        scr = W + 19200
        self.rmsnorm(pcol, scr)
        wg_d, wu_d, wd_d = self.w[name + "_g"], self.w[name + "_u"], self.w[name + "_d"]
        NFB = FF // 256
        for half in range(2):
            t0 = half * 1024
            for fb in range(NFB):
                bg, bu = wgu[(fb % 2) * 2], wgu[(fb % 2) * 2 + 1]
                self.wload(bg, wg_d[l, :, fb * 256:(fb + 1) * 256], KC)
                self.wload(bu, wu_d[l, :, fb * 256:(fb + 1) * 256], KC)
                for fc in range(2):
                    for tt in range(2):
                        sl = slice(t0 + tt * 512, t0 + (tt + 1) * 512)
                        pg, pu = self.bank(), self.bank()
                        for k in range(KC):
                            S.mm(pg, bg[:, k, fc * 128:(fc + 1) * 128], self.uT[:, k, sl], start=(k == 0), stop=(k == KC - 1))
                        for k in range(KC):
                            S.mm(pu, bu[:, k, fc * 128:(fc + 1) * 128], self.uT[:, k, sl], start=(k == 0), stop=(k == KC - 1))
                        s_ = sg[tt]
                        S.act(s_, pg, AF.Silu)
                        S.tt("dve", hT[:, fb * 2 + fc, tt * 512:(tt + 1) * 512], s_, pu, ALU.mult)
            for dc in range(KC):
                b = wd[dc % 2]
                self.wload(b, wd_d[l, :, dc * 128:(dc + 1) * 128], 22)
                for tt in range(2):
                    sl = slice(t0 + tt * 512, t0 + (tt + 1) * 512)
                    p = self.bank()
                    for f in range(22):
                        S.mm(p, b[:, f, :], hT[:, f, tt * 512:(tt + 1) * 512], start=(f == 0), stop=(f == 21))
                    S.stt("dve", self.xT[:, dc, sl], p, 0.5, self.xT[:, dc, sl], ALU.mult, ALU.add)
                if store:
                    S.dma("sp", self.yout[dc * 128:(dc + 1) * 128, t0:t0 + 1024], self.xT[:, dc, t0:t0 + 1024])


    def stream(self, specs, bufs):
        st = {"n": 0}

        def get(i):
            while st["n"] < min(len(specs), i + len(bufs) - 1):
                src, nk = specs[st["n"]]
                self.wload(bufs[st["n"] % len(bufs)][:, 0:nk, :], src, nk)
                st["n"] += 1
            return bufs[i % len(bufs)]
        return get

    def proj(self, wbuf, nk, rhs_of, sl):
        ps = self.bank()
        for k in range(nk):
            self.S.mm(ps, wbuf[:, k, :], rhs_of(k, sl), start=(k == 0), stop=(k == nk - 1))
        return ps

    def uk(self, k, sl):
        return self.uT[:, k, sl]

    def mixer(self, l):
        S = self.S
        W = A_WORK
        self.XH = A_XT + 8192
        self.rmsnorm(P_NM, W + 19200)
        for k in range(KC):
            S.dma("sp", self.xsp[k * 128:(k + 1) * 128, :], self.xT[:, k, :])
        self.Y = self.bv(A_XT, 16384).rearrange("p (k t) -> p k t", t=T)
        self.ACC = self.bv(W, 16384).rearrange("p (k t) -> p k t", t=T)
        WW = W + 8192
        self.wb = [self.bv(WW + i * 512, 1024).rearrange("p (k n) -> p k n", n=128) for i in range(4)]
        self.WP = WW + 2048
        stop = self.stop_after[1] if (self.stop_after and self.stop_after[0] == l) else None
        self.branch_C(l, NH=int(_os.environ.get("NH", "2")))
        if stop in ("mixC", "mixC_only"):
            return self.dump_Y(8)
        self.wb = [self.bv(WW + i * 512, 1024).rearrange("p (k n) -> p k n", n=128) for i in range(4)]
        self.merge(l, 2, 8, 1792, True)
        self.branch_A(l)
        self.merge(l, 0, 8, 0, False)
        self.branch_B(l)
        self.merge(l, 1, 6, 1024, False)
        self.outproj(l)

    def dbg(self, name, ap):
        S = self.S
        p, n = ap.shape
        dr = self.nc.dram_tensor("dbg_" + name, [p, n], F32, kind="ExternalOutput").ap()
        self.dbg_names.append("dbg_" + name)
        stg = self.fv(A_WORK + 21504, 512)
        for c0 in range(0, n, 512):
            c1 = min(n, c0 + 512)
            S.copy("dve", stg[0:p, 0:c1 - c0], ap[:, c0:c1])
            S.dma("sp", dr[:, c0:c1], stg[0:p, 0:c1 - c0])

    def dump_Y(self, n):
        S = self.S
        st = [self.fv(self.WP + i * 2048, 2048) for i in range(2)]
        for c in range(n):
            S.copy("dve", st[c % 2], self.Y[:, c, :])
            S.dma("sp", self.xsp[c * 128:(c + 1) * 128, :], st[c % 2])
        for c in range(KC):
            S.dma("sp", self.xT[:, c, :], self.xsp[c * 128:(c + 1) * 128, :])

    def merge(self, l, b, nkc, row0, first):
        S = self.S
        specs = []
        for dc in range(KC):
            specs.append((self.w_br[l, row0:row0 + nkc * 128, dc * 128:(dc + 1) * 128], nkc))
            specs.append((self.w_in[l, :, O_MG + b * 1024 + dc * 128:O_MG + b * 1024 + (dc + 1) * 128], KC))
        get = self.stream(specs, self.wb)
        sg = [self.fv(self.XH + i * 512, 512) for i in range(2)]
        tmp = [self.fv(self.XH + 1024 + i * 512, 512) for i in range(2)]
        for dc in range(KC):
            wz, wg = get(2 * dc), get(2 * dc + 1)
            for t in range(4):
                sl = slice(t * 512, (t + 1) * 512)
                pz = self.proj(wz, nkc, lambda k, s: self.Y[:, k, s], sl)
                pg = self.proj(wg, KC, self.uk, sl)
                S.act(sg[t % 2], pg, AF.Sigmoid)
                if first:
                    S.tt("dve", self.ACC[:, dc, sl], sg[t % 2], pz, ALU.mult)
                else:
                    S.tt("dve", tmp[t % 2], sg[t % 2], pz, ALU.mult)
                    S.tt("pool", self.ACC[:, dc, sl], self.ACC[:, dc, sl], tmp[t % 2], ALU.add)

    def outproj(self, l):
        S = self.S
        specs = [(self.w_out[l, :, d2 * 128:(d2 + 1) * 128], KC) for d2 in range(KC)]
        get = self.stream(specs, self.wb)
        xold = [self.fv(self.WP + i * 2048, 2048) for i in range(2)]
        S.dma("sp", xold[0], self.xsp[0:128, :])
        for d2 in range(KC):
            if d2 + 1 < KC:
                S.dma("sp", xold[(d2 + 1) % 2], self.xsp[(d2 + 1) * 128:(d2 + 2) * 128, :])
            wo = get(d2)
            for t in range(4):
                sl = slice(t * 512, (t + 1) * 512)
                p = self.proj(wo, KC, lambda k, s: self.ACC[:, k, s], sl)
                S.tt("dve", self.xT[:, d2, sl], p, xold[d2 % 2][:, sl], ALU.add)

    def branch_A(self, l):
        S, PV, Y = self.S, self.PV, self.Y
        rgw = self.fv(self.WP, 2048)
        S.dma("sp", rgw, self.rgw[l, :, :])
        rgw = rgw.rearrange("p (g c n) -> p g c n", g=2, c=8)
        der = self.fv(A_DER, 64)
        e, cl, cl2 = der[:, 0:8], der[:, 8:16], der[:, 16:24]
        S.act(e, PV[:, P_LAM:P_LAM + 8], AF.Exp, scale=-1.0)
        S.act(e, e, AF.Ln, bias=1.0)
        S.ts("dve", cl, e, -8.0, None, ALU.mult)
        S.ts("dve", cl2, e, -16.0, None, ALU.mult)
        col = lambda base_, c: PV[:, base_ + c:base_ + c + 1]
        bases = [self.XH, self.WP + 2048]
        streams = []
        for si in range(2):
            base = [bases[si]]

            def nb(n=512, base=base):
                v = self.fv(base[0], n)
                base[0] += n
                return v
            b = {"xin": [nb(520)[:, 0:515] for _ in range(2)]}
            for nm in ("xa", "r", "i", "a", "b", "gl"):
                b[nm] = nb()
            b["hb"] = [nb(), nb()]
            b["bk"] = [self.ps[:, (4 * si + j) * 512:(4 * si + j + 1) * 512] for j in range(4)]
            streams.append(b)

        def unit(c, t, b, bx, bg):
            sl = slice(t * 512, (t + 1) * 512)
            px, pg, pr, pi = b["bk"]
            xin, xa, r_, i_, a_, b_, gl, hb = b["xin"], b["xa"], b["r"], b["i"], b["a"], b["b"], b["gl"], b["hb"]
            for k in range(KC):
                S.mm(px, bx[:, k, :], self.uT[:, k, sl], start=(k == 0), stop=(k == KC - 1))
            for k in range(KC):
                S.mm(pg, bg[:, k, :], self.uT[:, k, sl], start=(k == 0), stop=(k == KC - 1))
            yield
            xi = xin[t % 2]
            if t == 0:
                S.memset("pool", xi[:, 0:3], 0.0)
            else:
                S.copy("pool", xi[:, 0:3], xin[(t - 1) % 2][:, 512:515])
            S.copy("act", xi[:, 3:515], px)
            S.act(gl, pg, AF.Gelu_apprx_tanh)
            S.ts("dve", xa, xi[:, 3:515], col(P_RCW + 24, c), col(P_RCB, c), ALU.mult, ALU.add)
            for k in (2, 1, 0):
                S.stt("dve", xa, xi[:, k:k + 512], col(P_RCW + 8 * k, c), xa, ALU.mult, ALU.add)
            yield
            S.mm(pr, rgw[:, 0, c, :], xa)
            S.mm(pi, rgw[:, 1, c, :], xa)
            yield
            S.act(r_, pr, AF.Sigmoid, bias=col(P_RBR, c))
            S.act(i_, pi, AF.Sigmoid, bias=col(P_RBI, c))
            S.act(a_, r_, AF.Exp, scale=cl[:, c:c + 1])
            S.act(b_, r_, AF.Exp, scale=cl2[:, c:c + 1])
            yield
            S.ts("dve", b_, b_, -1.0, 1.0, ALU.mult, ALU.add)
            S.ts("dve", b_, b_, 1e-20, None, ALU.max)
            S.act(b_, b_, AF.Ln)
            S.act(b_, b_, AF.Exp, scale=0.5)
            S.tt("dve", i_, i_, xa, ALU.mult)
            yield
            S.tt("dve", b_, b_, i_, ALU.mult)
            h = hb[t % 2]
            init = 0.0 if t == 0 else hb[(t - 1) % 2][:, 511:512]
            S.scan(h, a_, b_, init, ALU.mult, ALU.add)
            S.tt("dve", Y[:, c, sl], h, gl, ALU.mult)

        def chunk(c, b, bx, bg):
            for t in range(4):
                yield from unit(c, t, b, bx, bg)

        specs = []
        for c0 in range(0, 8, 2):
            for o in (O_RGX, O_RGG):
                for c in (c0, c0 + 1):
                    specs.append((self.w_in[l, :, o + c * 128:o + (c + 1) * 128], KC))
        for j, c0 in enumerate(range(0, 8, 2)):
            bufs = self.wb
            for i in range(4):
                self.wload(bufs[i][:, 0:KC, :], specs[4 * j + i][0], KC)
            interleave([chunk(c0 + si, streams[si], bufs[si], bufs[2 + si]) for si in range(2)])

    def branch_B(self, l):
        S, PV, Y, CF = self.S, self.PV, self.Y, self.CF
        WP, XH = self.WP, self.XH
        QT = self.bv(WP, 4096).rearrange("p (c t) -> p c t", t=T)
        KT = self.bv(WP + 2048, 4096).rearrange("p (c t) -> p c t", t=T)
        VG = self.bv(WP + 4096, 4096).rearrange("p (b n) -> p b n", n=256)
        DEN = self.fv(WP + 6144, 4096).rearrange("p (m t) -> p m t", t=T)
        WV = self.bv(WP + 10240, 2048).rearrange("p (k n) -> p k n", n=256)
        bank = lambda b: self.ps[:, b * 512:(b + 1) * 512]
        xb = [XH]

        def nb(n):
            v = self.fv(xb[0], n)
            xb[0] += n
            return v
        qs_ = []
        for si in range(2):
            qs_.append({"raw": nb(512), "sq": nb(256).bitcast(BF16), "rs": nb(512), "pp": bank(2 * si), "p2": bank(2 * si + 1)})
        bs_ = []
        for si in range(2):
            bs_.append({"sc": [nb(512), nb(512)], "E": [nb(256).bitcast(BF16), nb(256).bitcast(BF16)],
                        "bk": [bank(4 * si + j) for j in range(4)]})
        assert xb[0] <= XH + 8192
        for g, (win, dil) in enumerate(ATT_GROUPS):
            ci_specs = []
            for which in (O_AQ, O_AK):
                for jc in range(2):
                    c0 = which + g * 256 + jc * 128
                    ci_specs.append(self.w_in[l, :, c0:c0 + 128])
            for i in range(4):
                self.wload(self.wb[i][:, 0:KC, :], ci_specs[i], KC)
            self.wload(WV, self.w_in[l, :, O_AV + g * 256:O_AV + (g + 1) * 256], KC)
            nblk = 16 // dil

            def qk_stream(wi, jc, st):
                dstT, pn, eb = ((QT, P_AQN, float(np.log(0.125))), (KT, P_AKN, 0.0))[wi]
                wbuf = self.wb[wi * 2 + jc]
                gcol = PV[:, pn + 2 * g + jc:pn + 2 * g + jc + 1]
                raw, sq, rs, ps, ps2 = st["raw"], st["sq"], st["rs"], st["pp"], st["p2"]
                for t in range(4):
                    sl = slice(t * 512, (t + 1) * 512)
                    for k in range(KC):
                        S.mm(ps, wbuf[:, k, :], self.uT[:, k, sl], start=(k == 0), stop=(k == KC - 1))
                    yield
                    S.copy("act", raw, ps)
                    S.act(sq, ps, AF.Square)
                    S.mm(ps2, self.bd64_b, sq)
                    yield
                    S.act(rs, ps2, AF.Ln, bias=EPS, scale=1.0 / 64)
                    S.act(rs, rs, AF.Exp, scale=-0.5, bias=eb)
                    n_l = 512 // dil
                    dst = dstT[:, jc, :].rearrange("p (r l) -> p r l", r=dil)[:, :, t * n_l:(t + 1) * n_l]
                    S.stt("dve", dst, raw.rearrange("p (l r) -> p r l", r=dil), gcol,
                          rs.rearrange("p (l r) -> p r l", r=dil), ALU.mult, ALU.mult)

            def v_stream():
                for b in range(16):
                    r, n = divmod(b, nblk)
                    tok = slice(n * 128 * dil + r, n * 128 * dil + r + 127 * dil + 1, dil)
                    ps = bank(4 + b % 2)[:, 0:256]
                    for k in range(KC):
                        S.mm(ps, self.uT[:, k, tok], WV[:, k, :], start=(k == 0), stop=(k == KC - 1))
                    yield
                    S.copy("act", VG[:, b, :], ps)

            interleave([qk_stream(0, 0, qs_[0]), qk_stream(0, 1, qs_[1]), v_stream()])
            interleave([qk_stream(1, 0, qs_[0]), qk_stream(1, 1, qs_[1])])
            Bc = CF[:, C_AB + g * 1024:C_AB + g * 1024 + 512]
            Bp = CF[:, C_AB + g * 1024 + 512:C_AB + g * 1024 + 1024]

            def block(b, st):
                sc, E, bk = st["sc"], st["E"], st["bk"]
                r, n = divmod(b, nblk)
                tok = slice(n * 128 * dil + r, n * 128 * dil + r + 127 * dil + 1, dil)
                qs = slice(b * 128, (b + 1) * 128)
                ks = slice((b - 1) * 128, b * 128)
                for ei, (kk, Bt) in enumerate(((qs, Bc), (ks, Bp))):
                    if ei == 1 and n == 0:
                        continue
                    pcs = [bk[2 * ei], bk[2 * ei + 1]]
                    for j in range(4):
                        jc, hh = divmod(j, 2)
                        S.mm(pcs[hh][:, jc * 128:(jc + 1) * 128], KT[64 * hh:64 * hh + 64, jc, kk],
                             QT[64 * hh:64 * hh + 64, jc, qs])
                    yield
                    for hh in range(2):
                        S.tt("dve", sc[ei].rearrange("p (m h q) -> p h m q", m=2, h=2)[:, hh],
                             pcs[hh][:, 0:256].rearrange("p (m q) -> p m q", m=2),
                             Bt.rearrange("p (m h q) -> p h m q", m=2, h=2)[:, hh], ALU.add)
                    S.act(E[ei], sc[ei], AF.Exp)
                yield
                po, pd = bk[0], bk[1]
                for j in range(4):
                    m, hh = divmod(j, 2)
                    o_out = po[64 * hh:64 * hh + 64, m * 128:(m + 1) * 128]
                    d_out = pd[64 * hh:64 * hh + 64, m * 128:(m + 1) * 128]
                    ej = slice(j * 128, (j + 1) * 128)
                    if n > 0:
                        S.mm(o_out, VG[:, b - 1, j * 64:(j + 1) * 64], E[1][:, ej], start=True, stop=False)
                    S.mm(o_out, VG[:, b, j * 64:(j + 1) * 64], E[0][:, ej], start=(n == 0), stop=True)
                    if n > 0:
                        S.mm(d_out, self.ones_b[:, 0:64], E[1][:, ej], start=True, stop=False)
                    S.mm(d_out, self.ones_b[:, 0:64], E[0][:, ej], start=(n == 0), stop=True)
                yield
                po3 = po[:, 0:256].rearrange("p (m q) -> p m q", m=2)
                pd3 = pd[:, 0:256].rearrange("p (m q) -> p m q", m=2)
                S.copy("act", Y[:, 2 * g:2 * g + 2, tok], po3)
                if g == 0:
                    S.copy("dve", DEN[:, :, tok], pd3)
                else:
                    S.tt("dve", DEN[:, :, tok], DEN[:, :, tok], pd3, ALU.add)

            for b in range(0, 16, 2):
                interleave([block(b, bs_[0]), block(b + 1, bs_[1])])
        for m in range(2):
            S.recip(DEN[:, m, :], DEN[:, m, :])
        for c in range(6):
            S.tt("dve", Y[:, c, :], Y[:, c, :], DEN[:, c % 2, :], ALU.mult)

    def branch_C(self, l, NH=2):
        S, PV, Y, CF = self.S, self.PV, self.Y, self.CF
        W, XH = A_WORK, self.XH
        ident = self.ident
        UT = CF[:, C_UT:C_UT + 128]
        self.wb = [self.bv(W + i * 512, 1024).rearrange("p (k n) -> p k n", n=128) for i in range(4)]
        wbase = [W + 2048]

        def wal(n):
            v = self.fv(wbase[0], n)
            wbase[0] += n
            return v
        slots = []
        for s_ in range(2):
            d = {}
            d["kT"], d["qT"] = wal(2048), wal(2048)
            d["Ktok"] = wal(2048).rearrange("p (n d) -> p n d", d=128)
            d["Vtok"] = wal(1024).bitcast(BF16).rearrange("p (n d) -> p n d", d=128)
            d["S2"] = [wal(128), wal(128)]
            d["sqo"] = wal(64).bitcast(BF16)
            slots.append(d)
        BGf = wal(256)
        BG = BGf.rearrange("p (n c) -> p n c", c=16)
        GAM = wal(128).rearrange("p (n c) -> p n c", c=8)
        NB = wal(128).rearrange("p (n c) -> p n c", c=8)

        def mk_cs(al):
            c = {"xin": [al(520)[:, 0:515] for _ in range(2)]}
            c["cbuf"], c["sbuf"], c["rs"] = al(512), al(512), al(512)
            c["sq"] = al(256).bitcast(BF16)
            return c
        css = [mk_cs(wal)]
        colsb = wal(32)
        assert wbase[0] <= W + 22016, wbase[0] - W
        cbuf = css[0]["cbuf"]
        der = self.fv(A_DER, 64)
        nea = der[:, 24:32]
        xbase = [XH]

        def nb(n=128):
            v = self.fv(xbase[0], n)
            xbase[0] += n
            return v
        bank = lambda b: self.ps[:, b * 512:b * 512 + 128]
        lanes = []
        for li in range(2):
            ln = {}
            for nm in ("Gbc", "tmpT", "tmp2", "Ebc", "Aneg", "Fneg", "Td", "Wm"):
                ln[nm] = nb()
            ln["DT"], ln["dec"] = ln["tmpT"], ln["tmp2"]
            for nm in ("M", "MT", "XTn"):
                ln[nm] = [nb(), nb()]
            ln["tcol"], ln["kdc"] = colsb[:, 2 * li:2 * li + 1], colsb[:, 2 * li + 1:2 * li + 2]
            ln["L"] = [bank(2 * li), bank(2 * li + 1)]
            lanes.append(ln)
        ring = {nm: [nb(), nb(), nb()] for nm in ("TT", "QKT", "KgT", "QgT", "Kdec", "Vb")}
        ring["cdc"] = [colsb[:, 8 + i:9 + i] for i in range(3)]
        recp = {nm: nb() for nm in ("Rb", "ub", "osb", "rso")}
        assert xbase[0] <= XH + 8192, xbase[0] - XH
        RB = [bank(4), bank(5)]
        HB = [self.ps[:, 6 * 512:7 * 512], self.ps[:, 7 * 512:8 * 512]]

        def ctx(d, n, lane=None):
            dd = dict(d)
            dd.update(ring)
            if lane is not None:
                dd.update(lane)
                L = lane["L"]
            else:
                dd.update(recp)
                L = [None, None]
            dd["pt"] = [L[0], L[1], L[0], L[1], L[0], L[1], L[0], RB[0], RB[1], RB[0], RB[1], RB[0]]
            return dd

        WBA = self.wb[0]
        self.wload(WBA, self.w_in[l, :, O_BETA:O_BETA + 128], KC)
        pbg = self.bank()
        for n in range(16):
            for k in range(KC):
                S.mm(pbg[:, n * 16:(n + 1) * 16], self.uT[:, k, n * 128:(n + 1) * 128], WBA[:, k, 0:16],
                     start=(k == 0), stop=(k == KC - 1))
        pbg3 = pbg[:, 0:256].rearrange("p (n c) -> p n c", c=16)
        S.act(nea, PV[:, P_ALOG:P_ALOG + 8], AF.Exp)
        S.ts("dve", nea, nea, -1.0, None, ALU.mult)
        xg = cbuf[:, 0:256]
        xg3 = xg.rearrange("p (n c) -> p n c", c=16)
        S.act(BGf, pbg[:, 0:256], AF.Sigmoid)
        S.memset("dve", xg, 1.0)
        for h in range(8):
            S.act(xg3[:, :, 8 + h], pbg3[:, :, 8 + h], AF.Exp, bias=PV[:, P_DTB + h:P_DTB + h + 1])
        S.act(xg, xg, AF.Ln, bias=1.0)
        for h in range(8):
            S.ts("dve", BG[:, :, 8 + h], xg3[:, :, 8 + h], nea[:, h:h + 1], None, ALU.mult)
        for h in range(8):
            S.ts("dve", NB[:, :, h], BG[:, :, h], -1.0, None, ALU.mult)
        pgam = self.bank()
        S.mm(pgam[:, 0:256], UT, BGf)
        pgam3 = pgam[:, 0:256].rearrange("p (n c) -> p n c", c=16)
        for h in range(8):
            S.copy("act", GAM[:, :, h], pgam3[:, :, 8 + h])

        col = lambda j, k: PV[:, P_DCW + k * 24 + j:P_DCW + k * 24 + j + 1]

        def convsilu(ps, j, t, cs):
            xin, cb = cs["xin"], cs["cbuf"]
            xi = xin[t % 2]
            if t == 0:
                S.memset("pool", xi[:, 0:3], 0.0)
            else:
                S.copy("pool", xi[:, 0:3], xin[(t - 1) % 2][:, 512:515])
            S.copy("act", xi[:, 3:515], ps)
            S.ts("dve", cb, xi[:, 3:515], col(j, 3), None, ALU.mult)
            for k in (2, 1, 0):
                S.stt("dve", cb, xi[:, k:k + 512], col(j, k), cb, ALU.mult, ALU.add)
            S.act(cs["sbuf"], cb, AF.Silu)

        def head_prep(h, d, get, gbase, i, cs):
            kT, qT, Ktok, Vtok = d["kT"], d["qT"], d["Ktok"], d["Vtok"]
            bk = [HB[0], HB[0], HB[1], HB[1]]

            def proj_to(ps, wbuf, sl):
                for k in range(KC):
                    S.mm(ps, wbuf[:, k, :], self.uT[:, k, sl], start=(k == 0), stop=(k == KC - 1))
            wz = get(gbase + i)
            for t in range(4):
                sl = slice(t * 512, (t + 1) * 512)
                proj_to(bk[t % 2], wz, sl)
                yield
                S.act(Y[:, h, sl], bk[t % 2], AF.Silu)
            lnq = float(np.log(128.0 ** -0.5))
            for stage, (jofs, dst, eb, tok3) in enumerate(((0, qT, lnq, None), (8, kT, 0.0, Ktok), (16, None, None, Vtok))):
                wbuf = get(gbase + (stage + 1) + i)
                for t in range(4):
                    sl = slice(t * 512, (t + 1) * 512)
                    ps = bk[t % 2]
                    proj_to(ps, wbuf, sl)
                    yield
                    convsilu(ps, jofs + h, t, cs)
                    yield
                    if dst is not None:
                        S.act(cs["sq"], cs["sbuf"], AF.Square)
                        S.mm(bk[2], self.ones_b, cs["sq"])
                        yield
                        S.act(cs["rs"], bk[2], AF.Ln, bias=EPS)
                        S.act(cs["rs"], cs["rs"], AF.Exp, scale=-0.5, bias=eb)
                        S.tt("dve", dst[:, sl], cs["sbuf"], cs["rs"], ALU.mult)
                        src = dst[:, sl]
                    else:
                        src = cs["sbuf"]
                    if tok3 is not None:
                        for ii in range(4):
                            S.tr(bk[3][:, ii * 128:(ii + 1) * 128], src[:, ii * 128:(ii + 1) * 128], ident)
                        yield
                        S.copy("act", tok3[:, 4 * t:4 * t + 4, :], bk[3].rearrange("p (n d) -> p n d", d=128))

        def prep(h, n, d):
            par = n % 3
            sl = slice(n * 128, (n + 1) * 128)
            kT, qT, Ktok, Vtok = d["kT"], d["qT"], d["Ktok"], d["Vtok"]
            M, MT, XTn = d["M"], d["MT"], d["XTn"]
            gc, bc = BG[:, n, 8 + h:9 + h], BG[:, n, h:h + 1]
            nbc, gam = NB[:, n, h:h + 1], GAM[:, n, h:h + 1]
            S.ts("dve", d["Gbc"], CF[:, C_ONES:C_ONES + 128], gc, None, ALU.mult)
            pG = d["pt"][0]
            S.mm(pG, d["Gbc"], UT)
            pKK = d["pt"][1]
            S.mm(pKK, kT[:, sl], kT[:, sl])
            yield
            S.stt("dve", d["tmpT"], pG, gam, CF[:, C_NEGUT:C_NEGUT + 128], ALU.subtract, ALU.add)
            S.act(d["DT"], d["tmpT"], AF.Exp)
            S.stt("dve", d["tmp2"], pG, gam, CF[:, C_NEGLT:C_NEGLT + 128], ALU.subtract, ALU.subtract)
            S.act(d["dec"], d["tmp2"], AF.Exp, scale=-1.0)
            S.act(d["Ebc"], pG, AF.Exp)
            S.tt("dve", d["tcol"], pG[:, 127:128], gam, ALU.subtract)
            S.act(d["kdc"], d["tcol"], AF.Exp)
            S.act(d["cdc"][par], pG[:, 127:128], AF.Exp)
            pKQ = d["pt"][2]
            S.mm(pKQ, kT[:, sl], qT[:, sl])
            yield
            S.stt("dve", d["Aneg"], pKK, nbc, d["dec"], ALU.mult, ALU.mult)
            S.tt("pool", M[0], d["Aneg"], CF[:, C_BD:C_BD + 128], ALU.mult)
            S.tt("pool", d["Fneg"], d["Aneg"], CF[:, C_OFF:C_OFF + 128], ALU.mult)
            S.tt("dve", d["QKT"][par], pKQ, d["DT"], ALU.mult)
            S.tt("pool", d["KgT"][par], kT[:, sl], d["Ebc"], ALU.mult)
            S.tt("pool", d["QgT"][par], qT[:, sl], d["Ebc"], ALU.mult)
            S.ts("pool", d["Kdec"][par], Ktok[:, n, :], d["kdc"], None, ALU.mult)
            S.ts("pool", d["Vb"][par], Vtok[:, n, :], bc, None, ALU.mult)
            pT = d["pt"][3]
            S.tr(pT, M[0], ident)
            yield
            S.copy("act", MT[0], pT)
            S.tt("dve", XTn[0], MT[0], ident, ALU.add)
            for k in range(5):
                a, b = k % 2, (k + 1) % 2
                pM = d["pt"][4]
                S.mm(pM, MT[a], M[a])
                if k < 4:
                    pMT = d["pt"][5]
                    S.mm(pMT, M[a], MT[a])
                yield
                S.copy("act", M[b], pM)
                if k < 4:
                    S.copy("dve", MT[b], pMT)
                pX = d["pt"][6]
                S.mm(pX, M[b], XTn[a])
                yield
                S.tt("dve", XTn[b], XTn[a], pX, ALU.add)
            TdT = XTn[1]
            pTd = d["pt"][3]
            S.tr(pTd, TdT, ident)
            pW = d["pt"][4]
            S.mm(pW, d["Fneg"], TdT)
            yield
            S.copy("act", d["Td"], pTd)
            S.copy("dve", d["Wm"], pW)
            pP = d["pt"][5]
            S.mm(pP, d["Td"], d["Wm"])
            yield
            S.tt("dve", d["TT"][par], TdT, pP, ALU.add)

        def rec(h, n, d):
            par = n % 3
            sl = slice(n * 128, (n + 1) * 128)
            nbc = NB[:, n, h:h + 1]
            Sold, Snew = d["S2"][n % 2], d["S2"][(n + 1) % 2]
            Rb, ub, osb, rso, sqo = d["Rb"], d["ub"], d["osb"], d["rso"], d["sqo"]
            if n > 0:
                pKS = d["pt"][7]
                S.mm(pKS, d["KgT"][par], Sold)
                yield
                S.stt("dve", Rb, pKS, nbc, d["Vb"][par], ALU.mult, ALU.add)
                Rv = Rb
            else:
                Rv = d["Vb"][par]
            pu = d["pt"][8]
            S.mm(pu, d["TT"][par], Rv)
            yield
            S.copy("act", ub, pu)
            po = d["pt"][9]
            if n > 0:
                S.mm(po, Sold, d["QgT"][par], start=True, stop=False)
            S.mm(po, ub, d["QKT"][par], start=(n == 0), stop=True)
            pS = d["pt"][10]
            S.mm(pS, d["Kdec"][par], ub)
            yield
            if n > 0:
                S.ts("dve", Snew, Sold, d["cdc"][par], None, ALU.mult)
                S.tt("dve", Snew, pS, Snew, ALU.add)
            else:
                S.copy("dve", Snew, pS)
            S.act(sqo, po, AF.Square)
            S.copy("act", osb, po)
            ps2 = d["pt"][11]
            S.mm(ps2, self.ones_b, sqo)
            yield
            S.act(rso, ps2, AF.Ln, bias=EPS, scale=1.0 / 128)
            S.act(rso, rso, AF.Exp, scale=-0.5)
            S.stt("dve", osb, osb, PV[:, P_DON + h:P_DON + h + 1], rso, ALU.mult, ALU.mult)
            S.tt("dve", Y[:, h, sl], Y[:, h, sl], osb, ALU.mult)

        specs = []
        for h in range(8):
            for o in (O_DZ, O_DQ, O_DK, O_DV):
                specs.append((self.w_in[l, :, o + h * 128:o + (h + 1) * 128], KC))
        get = self.stream(specs, self.wb)

        def run_head(h, d, extra):
            active = {}
            if extra is not None:
                active["hp"] = extra
            fin_prep, fin_rec = set(), set()
            st = {"np": 0, "nr": 0, "rec_busy": False, "lane": [False, False]}
            while True:
                n = st["np"]
                if n < 16 and not st["lane"][n % 2] and (n < 3 or (n - 3) in fin_rec):
                    active[("p", n)] = prep(h, n, ctx(d, n, lanes[n % 2]))
                    st["lane"][n % 2] = True
                    st["np"] += 1
                    continue
                if not st["rec_busy"] and st["nr"] < 16 and st["nr"] in fin_prep:
                    active[("r", st["nr"])] = rec(h, st["nr"], ctx(d, st["nr"]))
                    st["rec_busy"] = True
                if not active:
                    break
                prio = lambda k_: 2 if k_ == "hp" else (0 if k_[0] == "r" else 1)
                for key in sorted(active, key=lambda k_: (prio(k_), 0 if k_ == "hp" else k_[1])):
                    try:
                        next(active[key])
                    except StopIteration:
                        del active[key]
                        if key == "hp":
                            pass
                        elif key[0] == "p":
                            fin_prep.add(key[1])
                            st["lane"][key[1] % 2] = False
                        else:
                            fin_rec.add(key[1])
                            st["rec_busy"] = False
                            st["nr"] += 1

        interleave([head_prep(0, slots[0], get, 0, 0, css[0])])
        for h in range(8):
            nxt = head_prep(h + 1, slots[(h + 1) % 2], get, 4 * (h + 1), 0, css[0]) if h + 1 < 8 else None
            run_head(h, slots[h % 2], nxt)


def host_tables():
    cf = np.zeros((128, NCF), np.float32)
    i = np.arange(128)
    cf[:, C_ID:C_ID + 128] = np.eye(128, dtype=np.float32)
    P, Fr = np.meshgrid(i, i, indexing="ij")
    cf[:, C_NEGUT:C_NEGUT + 128] = np.where(Fr >= P, 0.0, NEG)
    cf[:, C_NEGLT:C_NEGLT + 128] = np.where(Fr < P, 0.0, NEG)
    cf[:, C_BD:C_BD + 128] = ((P // 64) == (Fr // 64)).astype(np.float32)
    cf[:, C_OFF:C_OFF + 128] = ((P >= 64) & (Fr < 64)).astype(np.float32)
    cf[:, C_UT:C_UT + 128] = (P <= Fr).astype(np.float32)
    cf[:, C_ONES:C_ONES + 128] = 1.0
    h = np.arange(1, 13, dtype=np.float32)
    slopes = np.exp2(np.float32(-8.0) * h / np.float32(12)).astype(np.float32)
    for g, (win, dil) in enumerate(ATT_GROUPS):
        for j in range(4):
            sl = slopes[g * 4 + j]
            k_, q_ = P, Fr
            dcur = (q_ - k_).astype(np.float32)
            cur = np.where(k_ <= q_, -sl * (dcur * dil), NEG)
            dprev = (q_ + 128 - k_).astype(np.float32)
            prev = np.where(k_ >= q_, -sl * (dprev * dil), NEG)
            base = C_AB + g * 1024
            cf[:, base + j * 128: base + (j + 1) * 128] = cur
            cf[:, base + 512 + j * 128: base + 512 + (j + 1) * 128] = prev
    cb = np.zeros((128, 256), np.float32)
    cb[:, 0:128] = 1.0
    cb[:, 128:256] = ((P // 64) == (Fr // 64)).astype(np.float32)
    return cf, cb


def host_params(inp):
    pv = np.zeros((L, 128, NPV), np.float32)
    col = lambda v: np.ascontiguousarray(np.asarray(v, np.float32).reshape(-1, 128).T)
    for l in range(L):
        pv[l, :, P_N1:P_N1 + 8] = col(inp["ffn1_norm"][l])
        pv[l, :, P_NM:P_NM + 8] = col(inp["mix_norm"][l])
        pv[l, :, P_N2:P_N2 + 8] = col(inp["ffn2_norm"][l])
        for k in range(4):
            pv[l, :, P_RCW + k * 8:P_RCW + (k + 1) * 8] = col(inp["rg_conv_w"][l, k])
            pv[l, :, P_DCW + k * 24:P_DCW + (k + 1) * 24] = col(inp["dn_conv_w"][l, k])
        pv[l, :, P_RCB:P_RCB + 8] = col(inp["rg_conv_b"][l])
        pv[l, :, P_RBR:P_RBR + 8] = col(inp["rg_b_r"][l])
        pv[l, :, P_RBI:P_RBI + 8] = col(inp["rg_b_i"][l])
        pv[l, :, P_LAM:P_LAM + 8] = col(inp["rg_lambda"][l])
        pv[l, :, P_AQN:P_AQN + 6] = col(inp["att_q_norm"][l])
        pv[l, :, P_AKN:P_AKN + 6] = col(inp["att_k_norm"][l])
        pv[l, :, P_DON:P_DON + 8] = col(inp["dn_out_norm"][l])
        pv[l, :, P_ALOG:P_ALOG + 8] = np.broadcast_to(np.asarray(inp["dn_a_log"][l], np.float32)[None, :], (128, 8))
        pv[l, :, P_DTB:P_DTB + 8] = np.broadcast_to(np.asarray(inp["dn_dt_bias"][l], np.float32)[None, :], (128, 8))
    rgw = np.zeros((L, 128, 2, 8, 128), np.float32)
    for l in range(L):
        for gi, nm in enumerate(("rg_w_r", "rg_w_i")):
            w = np.asarray(inp[nm][l], np.float32)
            for c in range(8):
                rgw[l, 0:64, gi, c, 0:64] = w[2 * c]
                rgw[l, 64:128, gi, c, 64:128] = w[2 * c + 1]
    return pv, rgw.reshape(L, 128, 2048)


_CACHE = {}


def run(inputs, nlayers=L, stop_after=None, ncores=8, dbgs=()):
    key = (nlayers, stop_after, tuple(dbgs))
    if key not in _CACHE:
        b_ = Builder(nlayers, stop_after)
        b_.dbgs = set(dbgs)
        _CACHE[key] = (b_.build(), b_)
    nc, b_ = _CACHE[key]
    cf, cb = host_tables()
    pv, rgw = host_params(inputs)
    x = np.asarray(inputs["x"], np.float32)
    shared = {"cf": cf, "cb": cb, "pv": pv, "rgw": rgw}
    for f in ("ffn1", "ffn2"):
        for s in ("w_gate", "w_up", "w_down"):
            shared[f + "_" + s] = np.ascontiguousarray(np.asarray(inputs[f + "_" + s], np.float32))
    for nm in ("w_in", "w_branch", "w_out"):
        shared[nm] = np.ascontiguousarray(np.asarray(inputs[nm], np.float32))
    in_maps = []
    for b in range(ncores):
        m = dict(shared)
        m["xT"] = np.ascontiguousarray(x[b].T)
        in_maps.append(m)
    res = run_bass_kernel_spmd(nc, in_maps, core_ids=list(range(ncores)))
    if b_.dbg_names:
        run.dbg = {n: res.results[0][n] for n in b_.dbg_names}
    return np.stack([np.ascontiguousarray(r["y"].T) for r in res.results], axis=0)


def kernel(**inputs):
    return run(inputs).astype(np.float32)
```

```python
import contextlib
import os as _os
import numpy as np
import concourse.bass as bass
import concourse.mybir as mybir
from concourse.bass_utils import run_bass_kernel_spmd

F32 = mybir.dt.float32
BF16 = mybir.dt.bfloat16
AF = mybir.ActivationFunctionType
ALU = mybir.AluOpType
ESZ = {F32: 4, BF16: 2}

T = 2048
D = 1024
FF = 2816
L = 2
KC = 8
IN_DIM = 11536
O_RGX, O_RGG, O_AQ, O_AK, O_AV = 0, 1024, 2048, 2816, 3584
O_DQ, O_DK, O_DV, O_DZ, O_BETA, O_ALPHA, O_MG = 4352, 5376, 6400, 7424, 8448, 8456, 8464
EPS = 1e-6
NEG = -30000.0
ATT_GROUPS = ((128, 1), (512, 4), (2048, 16))

P_N1, P_NM, P_N2, P_RCW, P_RCB, P_RBR, P_RBI, P_LAM = 0, 8, 16, 24, 56, 64, 72, 80
P_AQN, P_AKN, P_DCW, P_DON, P_ALOG, P_DTB = 88, 94, 100, 196, 204, 212
NPV = 224
C_ID, C_NEGUT, C_NEGLT, C_BD, C_OFF, C_UT, C_ONES, C_AB = 0, 128, 256, 384, 512, 640, 768, 896
NCF = 896 + 3072
A_XT, A_UT, A_CF, A_CB, A_PV, A_DER, A_WORK, A_END = 0, 16384, 24576, 28544, 28672, 29120, 29184, 51200


def _box(ap):
    es = ESZ[ap.dtype]
    pat = ap.ap
    space = str(ap.space)
    if space == "DRAM":
        return ("DRAM", ap.tensor.name, 0, 1, 0, 1)
    pstep, pcnt = pat[0]
    if pstep == 0:
        row = int(np.prod(ap.tensor.shape[1:]))
        pcnt = 1
    else:
        row = pstep
    p0 = ap.offset // row
    c0 = ap.offset % row
    ext = 1
    for st, cnt in pat[1:]:
        ext += (cnt - 1) * abs(st)
    return (space, ap.tensor.name, p0, p0 + pcnt, c0 * es, (c0 + ext) * es)


class Sched:
    ENGS = ("pe", "act", "dve", "pool", "sp")
    BUCKET = 256
    EPOCH = 4000
    NDMA = 24

    def __init__(self, nc):
        self.nc = nc
        self.ops = {e: [] for e in self.ENGS}
        self.tab = {}
        self.ndma = {e: 0 for e in self.ENGS}

    def _keys(self, box):
        space, name, p0, p1, b0, b1 = box
        if space == "DRAM":
            return [(space, name, 0, 0)]
        if space == "PSUM":
            return [(space, name, 0, b) for b in range(b0 // 2048, (b1 - 1) // 2048 + 1)]
        ks = []
        for q in range(p0 // 32, (p1 - 1) // 32 + 1):
            for b in range(b0 // self.BUCKET, (b1 - 1) // self.BUCKET + 1):
                ks.append((space, name, q, b))
        return ks

    def op(self, eng, fn, reads, writes, dma=False):
        idx = len(self.ops[eng])
        if dma:
            tok = ("dma", eng, self.ndma[eng])
            self.ndma[eng] += 1
        else:
            tok = (eng, idx)
        deps = set()
        rk, wk = [], []
        for ap in reads:
            rk += self._keys(_box(ap))
        for ap in writes:
            wk += self._keys(_box(ap))
        for k in rk:
            ent = self.tab.get(k)
            if ent is not None and ent[0] is not None:
                deps.add(ent[0])
        for k in wk:
            ent = self.tab.get(k)
            if ent is not None:
                if ent[0] is not None:
                    deps.add(ent[0])
                deps.update(ent[1].values())
        for k in rk:
            ent = self.tab.setdefault(k, [None, {}])
            ent[1][tok if dma else tok[:-1]] = tok
        for k in wk:
            self.tab[k] = [tok, {}]
        deps.discard(tok)
        self.ops[eng].append({"fn": fn, "deps": deps, "dma": tok if dma else None})
        return tok

    def mm(self, out, lhsT, rhs, start=True, stop=True):
        return self.op("pe", lambda e: e.matmul(out, lhsT, rhs, start=start, stop=stop),
                       [lhsT, rhs] + ([] if start else [out]), [out])

    def tr(self, out, in_, ident):
        return self.op("pe", lambda e: e.transpose(out, in_, ident), [in_, ident], [out])

    def act(self, out, in_, func, bias=None, scale=None):
        kw = {}
        rd = [in_]
        if bias is not None:
            kw["bias"] = bias
            if not isinstance(bias, (int, float)):
                rd.append(bias)
        if scale is not None:
            kw["scale"] = scale
            if not isinstance(scale, (int, float)):
                rd.append(scale)
        return self.op("act", lambda e: e.activation(out, in_, func, **kw), rd, [out])

    def tt(self, eng, out, in0, in1, op):
        return self.op(eng, lambda e: e.tensor_tensor(out, in0, in1, op), [in0, in1], [out])

    def ts(self, eng, out, in0, s1, s2, op0, op1=None):
        rd = [in0] + [s for s in (s1, s2) if s is not None and not isinstance(s, (int, float))]
        if op1 is None:
            return self.op(eng, lambda e: e.tensor_scalar(out, in0, s1, None, op0), rd, [out])
        return self.op(eng, lambda e: e.tensor_scalar(out, in0, s1, s2, op0, op1), rd, [out])

    def stt(self, eng, out, in0, scalar, in1, op0, op1):
        rd = [in0, in1] + ([] if isinstance(scalar, (int, float)) else [scalar])
        return self.op(eng, lambda e: e.scalar_tensor_tensor(out, in0, scalar, in1, op0, op1), rd, [out])

    def copy(self, eng, out, in_):
        if eng == "act":
            return self.op(eng, lambda e: e.copy(out, in_), [in_], [out])
        return self.op(eng, lambda e: e.tensor_copy(out, in_), [in_], [out])

    def scan(self, out, d0, d1, init, op0, op1):
        rd = [d0, d1] + ([] if isinstance(init, (int, float)) else [init])
        return self.op("dve", lambda e: e.tensor_tensor_scan(out, d0, d1, init, op0, op1), rd, [out])

    def recip(self, out, in_):
        return self.op("dve", lambda e: e.reciprocal(out, in_), [in_], [out])

    def memset(self, eng, ap, val):
        return self.op(eng, lambda e: e.memset(ap, val), [], [ap])

    def dma(self, q, out, in_):
        return self.op(q, lambda e: e.dma_start(out=out, in_=in_), [in_], [out], dma=True)

    def emit(self):
        nc = self.nc
        engs = self.ENGS
        sig = {e: set() for e in engs}
        for f in engs:
            for rec in self.ops[f]:
                for d in rec["deps"]:
                    if d[0] == "dma":
                        continue
                    e, i = d
                    if e == f and e == "pe":
                        continue
                    sig[e].add(i)
        cnt = {e: {} for e in engs}
        for e in engs:
            for c, i in enumerate(sorted(sig[e])):
                cnt[e][i] = c + 1
        nep = {e: (len(cnt[e]) + self.EPOCH - 1) // self.EPOCH for e in engs}
        for e in engs:
            for i, rec in enumerate(self.ops[e]):
                rec["idx"] = i
        with contextlib.ExitStack() as st:
            esem = {e: [st.enter_context(nc.semaphore(f"s_{e}_{k}")) for k in range(nep[e])] for e in engs}
            dsem = {e: [st.enter_context(nc.semaphore(f"d_{e}_{k}")) for k in range(min(self.NDMA, self.ndma[e]))]
                    for e in engs}
            block = st.enter_context(nc.Block())

            def gen(f):
                def body(eobj):
                    waited = {}
                    for rec in self.ops[f]:
                        need = {}
                        for d in rec["deps"]:
                            if d[0] == "dma":
                                _, q, k = d
                                key = ("d", q, k % self.NDMA)
                                val = 16 * (k // self.NDMA + 1)
                            else:
                                e, i = d
                                if e == f and e == "pe":
                                    continue
                                c = cnt[e][i]
                                key = ("e", e, (c - 1) // self.EPOCH)
                                val = (c - 1) % self.EPOCH + 1
                            if need.get(key, 0) < val:
                                need[key] = val
                        if rec["dma"] is not None:
                            _, q, k = rec["dma"]
                            if k >= self.NDMA:
                                key = ("d", q, k % self.NDMA)
                                val = 16 * (k // self.NDMA)
                                if need.get(key, 0) < val:
                                    need[key] = val
                        for key, val in need.items():
                            if waited.get(key, 0) >= val:
                                continue
                            waited[key] = val
                            sem = dsem[key[1]][key[2]] if key[0] == "d" else esem[key[1]][key[2]]
                            eobj.wait_ge(sem, val)
                        ins = rec["fn"](eobj)
                        if rec["dma"] is not None:
                            _, q, k = rec["dma"]
                            ins.then_inc(dsem[q][k % self.NDMA], 16)
                        elif rec["idx"] in cnt[f]:
                            c = cnt[f][rec["idx"]]
                            ins.then_inc(esem[f][(c - 1) // self.EPOCH], 1)
                    if f == "sp":
                        for q in engs:
                            for s in range(min(self.NDMA, self.ndma[q])):
                                n = (self.ndma[q] - 1 - s) // self.NDMA + 1
                                eobj.wait_ge(dsem[q][s], 16 * n)
                return body

            block.tensor(gen("pe"))
            block.scalar(gen("act"))
            block.vector(gen("dve"))
            block.gpsimd(gen("pool"))
            block.sync(gen("sp"))


def interleave(gens):
    gens = list(gens)
    while gens:
        for g in list(gens):
            try:
                next(g)
            except StopIteration:
                gens.remove(g)


class Builder:
    def __init__(self, nlayers=L, stop_after=None):
        self.nlayers = nlayers
        self.stop_after = stop_after
        nc = bass.Bass("TRN2", target_bir_lowering=False)
        self.nc = nc
        dt = lambda name, shape, kind="ExternalInput": nc.dram_tensor(name, shape, F32, kind=kind).ap()
        self.xin = dt("xT", [D, T])
        self.cf = dt("cf", [128, NCF])
        self.cb = dt("cb", [128, 256])
        self.pv = dt("pv", [L, 128, NPV])
        self.rgw = dt("rgw", [L, 128, 2048])
        self.w = {}
        for f in ("ffn1", "ffn2"):
            self.w[f + "_g"] = dt(f + "_w_gate", [L, D, FF])
            self.w[f + "_u"] = dt(f + "_w_up", [L, D, FF])
            self.w[f + "_d"] = dt(f + "_w_down", [L, FF, D])
        self.w_in = dt("w_in", [L, D, IN_DIM])
        self.w_br = dt("w_branch", [L, FF, D])
        self.w_out = dt("w_out", [L, D, D])
        self.xsp = dt("xspill", [D, T], kind="Internal")
        self.yout = dt("y", [D, T], kind="ExternalOutput")
        self.psb = 0
        self.dbg_names = []
        self.dbgs = set()

    def fv(self, c0, n):
        return self.ar[:, c0:c0 + n]

    def bv(self, c0, nbf):
        return self.ar[:, c0:c0 + nbf // 2].bitcast(BF16)

    def bank(self):
        b = self.psb
        self.psb = (self.psb + 1) % 8
        return self.ps[:, b * 512:(b + 1) * 512]

    def build(self):
        nc = self.nc
        with (nc.sbuf_tensor("arena", [128, A_END], F32) as ar, nc.psum_tensor("psum", [128, 4096], F32) as ps):
            self.ar = ar
            self.ps = ps
            self.S = Sched(nc)
            self.program()
            self.S.emit()
        return nc

    def program(self):
        S = self.S
        self.xT = self.fv(A_XT, 16384).rearrange("p (k t) -> p k t", t=T)
        self.uT = self.bv(A_UT, 16384).rearrange("p (k t) -> p k t", t=T)
        self.CF = self.fv(A_CF, NCF)
        self.CB = self.bv(A_CB, 256)
        self.ident = self.CF[:, C_ID:C_ID + 128]
        self.ones_b = self.CB[:, 0:128]
        self.bd64_b = self.CB[:, 128:256]
        S.dma("sp", self.CF, self.cf[:, :])
        S.dma("pool", self.CB, self.cb[:, :])
        for l in range(L):
            S.dma("sp", self.fv(A_PV + l * NPV, NPV), self.pv[l, :, :])
        for k in range(KC):
            S.dma("sp", self.xT[:, k, :], self.xin[k * 128:(k + 1) * 128, :])
        done = False
        for l in range(self.nlayers):
            self.PV = self.fv(A_PV + l * NPV, NPV)
            if not (self.stop_after and self.stop_after[1].endswith("_only")):
                self.ffn(l, "ffn1", P_N1)
            if self.stop_after == (l, "ffn1"):
                done = True
                break
            self.mixer(l)
            if self.stop_after is not None and self.stop_after[0] == l and self.stop_after[1].startswith("mix"):
                done = True
                break
            last = (l == self.nlayers - 1)
            self.ffn(l, "ffn2", P_N2, store=last)
            if last:
                return
        for k in range(KC):
            S.dma("sp", self.yout[k * 128:(k + 1) * 128, :], self.xT[:, k, :])

    def rmsnorm(self, pcol, wbase):
        S = self.S
        for t in range(4):
            sl = slice(t * 512, (t + 1) * 512)
            ps = self.bank()
            R = self.fv(wbase + 1024 + (t % 2) * 512, 512)
            for k in range(KC):
                sq = self.bv(wbase + (k % 4) * 256, 512)
                S.act(sq, self.xT[:, k, sl], AF.Square)
                S.mm(ps, self.ones_b, sq, start=(k == 0), stop=(k == KC - 1))
            S.act(R, ps, AF.Ln, bias=EPS, scale=1.0 / D)
            S.act(R, R, AF.Exp, scale=-0.5)
            for k in range(KC):
                S.stt("dve", self.uT[:, k, sl], self.xT[:, k, sl], self.PV[:, pcol + k:pcol + k + 1], R,
                      ALU.mult, ALU.mult)

    def wload(self, dst, src2d, nk):
        self.S.dma("pool", dst, src2d.rearrange("(k p) n -> p k n", p=128))

    def ffn(self, l, name, pcol, store=False):
        S = self.S
        W = A_WORK
        hT = self.bv(W, 22 * 1024).rearrange("p (f t) -> p f t", t=1024)
        wgu = [self.bv(W + 11264 + i * 1024, 2048).rearrange("p (k n) -> p k n", n=256) for i in range(4)]
        wd = [self.bv(W + 15360 + i * 1408, 2816).rearrange("p (f n) -> p f n", n=128) for i in range(2)]
        sg = [self.fv(W + 18176 + i * 512, 512) for i in range(2)]
        scr = W + 19200
        self.rmsnorm(pcol, scr)
        wg_d, wu_d, wd_d = self.w[name + "_g"], self.w[name + "_u"], self.w[name + "_d"]
        NFB = FF // 256
        for half in range(2):
            t0 = half * 1024
            for fb in range(NFB):
                bg, bu = wgu[(fb % 2) * 2], wgu[(fb % 2) * 2 + 1]
                self.wload(bg, wg_d[l, :, fb * 256:(fb + 1) * 256], KC)
                self.wload(bu, wu_d[l, :, fb * 256:(fb + 1) * 256], KC)
                for fc in range(2):
                    for tt in range(2):
                        sl = slice(t0 + tt * 512, t0 + (tt + 1) * 512)
                        pg, pu = self.bank(), self.bank()
                        for k in range(KC):
                            S.mm(pg, bg[:, k, fc * 128:(fc + 1) * 128], self.uT[:, k, sl], start=(k == 0), stop=(k == KC - 1))
                        for k in range(KC):
                            S.mm(pu, bu[:, k, fc * 128:(fc + 1) * 128], self.uT[:, k, sl], start=(k == 0), stop=(k == KC - 1))
                        s_ = sg[tt]
                        S.act(s_, pg, AF.Silu)
                        S.tt("dve", hT[:, fb * 2 + fc, tt * 512:(tt + 1) * 512], s_, pu, ALU.mult)
            for dc in range(KC):
                b = wd[dc % 2]
                self.wload(b, wd_d[l, :, dc * 128:(dc + 1) * 128], 22)
                for tt in range(2):
                    sl = slice(t0 + tt * 512, t0 + (tt + 1) * 512)
                    p = self.bank()
                    for f in range(22):
                        S.mm(p, b[:, f, :], hT[:, f, tt * 512:(tt + 1) * 512], start=(f == 0), stop=(f == 21))
                    S.stt("dve", self.xT[:, dc, sl], p, 0.5, self.xT[:, dc, sl], ALU.mult, ALU.add)
                if store:
                    S.dma("sp", self.yout[dc * 128:(dc + 1) * 128, t0:t0 + 1024], self.xT[:, dc, t0:t0 + 1024])


    def stream(self, specs, bufs):
        st = {"n": 0}

        def get(i):
            while st["n"] < min(len(specs), i + len(bufs) - 1):
                src, nk = specs[st["n"]]
                self.wload(bufs[st["n"] % len(bufs)][:, 0:nk, :], src, nk)
                st["n"] += 1
            return bufs[i % len(bufs)]
        return get

    def proj(self, wbuf, nk, rhs_of, sl):
        ps = self.bank()
        for k in range(nk):
            self.S.mm(ps, wbuf[:, k, :], rhs_of(k, sl), start=(k == 0), stop=(k == nk - 1))
        return ps

    def uk(self, k, sl):
        return self.uT[:, k, sl]

    def mixer(self, l):
        S = self.S
        W = A_WORK
        self.XH = A_XT + 8192
        self.rmsnorm(P_NM, W + 19200)
        for k in range(KC):
            S.dma("sp", self.xsp[k * 128:(k + 1) * 128, :], self.xT[:, k, :])
        self.Y = self.bv(A_XT, 16384).rearrange("p (k t) -> p k t", t=T)
        self.ACC = self.bv(W, 16384).rearrange("p (k t) -> p k t", t=T)
        WW = W + 8192
        self.wb = [self.bv(WW + i * 512, 1024).rearrange("p (k n) -> p k n", n=128) for i in range(4)]
        self.WP = WW + 2048
        stop = self.stop_after[1] if (self.stop_after and self.stop_after[0] == l) else None
        self.branch_C(l, NH=int(_os.environ.get("NH", "2")))
        if stop in ("mixC", "mixC_only"):
            return self.dump_Y(8)
        self.wb = [self.bv(WW + i * 512, 1024).rearrange("p (k n) -> p k n", n=128) for i in range(4)]
        self.merge(l, 2, 8, 1792, True)
        self.branch_A(l)
        self.merge(l, 0, 8, 0, False)
        self.branch_B(l)
        self.merge(l, 1, 6, 1024, False)
        self.outproj(l)

    def dbg(self, name, ap):
        S = self.S
        p, n = ap.shape
        dr = self.nc.dram_tensor("dbg_" + name, [p, n], F32, kind="ExternalOutput").ap()
        self.dbg_names.append("dbg_" + name)
        stg = self.fv(A_WORK + 21504, 512)
        for c0 in range(0, n, 512):
            c1 = min(n, c0 + 512)
            S.copy("dve", stg[0:p, 0:c1 - c0], ap[:, c0:c1])
            S.dma("sp", dr[:, c0:c1], stg[0:p, 0:c1 - c0])

    def dump_Y(self, n):
        S = self.S
        st = [self.fv(self.WP + i * 2048, 2048) for i in range(2)]
        for c in range(n):
            S.copy("dve", st[c % 2], self.Y[:, c, :])
            S.dma("sp", self.xsp[c * 128:(c + 1) * 128, :], st[c % 2])
        for c in range(KC):
            S.dma("sp", self.xT[:, c, :], self.xsp[c * 128:(c + 1) * 128, :])

    def merge(self, l, b, nkc, row0, first):
        S = self.S
        specs = []
        for dc in range(KC):
            specs.append((self.w_br[l, row0:row0 + nkc * 128, dc * 128:(dc + 1) * 128], nkc))
            specs.append((self.w_in[l, :, O_MG + b * 1024 + dc * 128:O_MG + b * 1024 + (dc + 1) * 128], KC))
        get = self.stream(specs, self.wb)
        sg = [self.fv(self.XH + i * 512, 512) for i in range(2)]
        tmp = [self.fv(self.XH + 1024 + i * 512, 512) for i in range(2)]
        for dc in range(KC):
            wz, wg = get(2 * dc), get(2 * dc + 1)
            for t in range(4):
                sl = slice(t * 512, (t + 1) * 512)
                pz = self.proj(wz, nkc, lambda k, s: self.Y[:, k, s], sl)
                pg = self.proj(wg, KC, self.uk, sl)
                S.act(sg[t % 2], pg, AF.Sigmoid)
                if first:
                    S.tt("dve", self.ACC[:, dc, sl], sg[t % 2], pz, ALU.mult)
                else:
                    S.tt("dve", tmp[t % 2], sg[t % 2], pz, ALU.mult)
                    S.tt("pool", self.ACC[:, dc, sl], self.ACC[:, dc, sl], tmp[t % 2], ALU.add)

    def outproj(self, l):
        S = self.S
        specs = [(self.w_out[l, :, d2 * 128:(d2 + 1) * 128], KC) for d2 in range(KC)]
        get = self.stream(specs, self.wb)
        xold = [self.fv(self.WP + i * 2048, 2048) for i in range(2)]
        S.dma("sp", xold[0], self.xsp[0:128, :])
        for d2 in range(KC):
            if d2 + 1 < KC:
                S.dma("sp", xold[(d2 + 1) % 2], self.xsp[(d2 + 1) * 128:(d2 + 2) * 128, :])
            wo = get(d2)
            for t in range(4):
                sl = slice(t * 512, (t + 1) * 512)
                p = self.proj(wo, KC, lambda k, s: self.ACC[:, k, s], sl)
                S.tt("dve", self.xT[:, d2, sl], p, xold[d2 % 2][:, sl], ALU.add)

    def branch_A(self, l):
        S, PV, Y = self.S, self.PV, self.Y
        rgw = self.fv(self.WP, 2048)
        S.dma("sp", rgw, self.rgw[l, :, :])
        rgw = rgw.rearrange("p (g c n) -> p g c n", g=2, c=8)
        der = self.fv(A_DER, 64)
        e, cl, cl2 = der[:, 0:8], der[:, 8:16], der[:, 16:24]
        S.act(e, PV[:, P_LAM:P_LAM + 8], AF.Exp, scale=-1.0)
        S.act(e, e, AF.Ln, bias=1.0)
        S.ts("dve", cl, e, -8.0, None, ALU.mult)
        S.ts("dve", cl2, e, -16.0, None, ALU.mult)
        col = lambda base_, c: PV[:, base_ + c:base_ + c + 1]
        bases = [self.XH, self.WP + 2048]
        streams = []
        for si in range(2):
            base = [bases[si]]

            def nb(n=512, base=base):
                v = self.fv(base[0], n)
                base[0] += n
                return v
            b = {"xin": [nb(520)[:, 0:515] for _ in range(2)]}
            for nm in ("xa", "r", "i", "a", "b", "gl"):
                b[nm] = nb()
            b["hb"] = [nb(), nb()]
            b["bk"] = [self.ps[:, (4 * si + j) * 512:(4 * si + j + 1) * 512] for j in range(4)]
            streams.append(b)

        def unit(c, t, b, bx, bg):
            sl = slice(t * 512, (t + 1) * 512)
            px, pg, pr, pi = b["bk"]
            xin, xa, r_, i_, a_, b_, gl, hb = b["xin"], b["xa"], b["r"], b["i"], b["a"], b["b"], b["gl"], b["hb"]
            for k in range(KC):
                S.mm(px, bx[:, k, :], self.uT[:, k, sl], start=(k == 0), stop=(k == KC - 1))
            for k in range(KC):
                S.mm(pg, bg[:, k, :], self.uT[:, k, sl], start=(k == 0), stop=(k == KC - 1))
            yield
            xi = xin[t % 2]
            if t == 0:
                S.memset("pool", xi[:, 0:3], 0.0)
            else:
                S.copy("pool", xi[:, 0:3], xin[(t - 1) % 2][:, 512:515])
            S.copy("act", xi[:, 3:515], px)
            S.act(gl, pg, AF.Gelu_apprx_tanh)
            S.ts("dve", xa, xi[:, 3:515], col(P_RCW + 24, c), col(P_RCB, c), ALU.mult, ALU.add)
            for k in (2, 1, 0):
                S.stt("dve", xa, xi[:, k:k + 512], col(P_RCW + 8 * k, c), xa, ALU.mult, ALU.add)
            yield
            S.mm(pr, rgw[:, 0, c, :], xa)
            S.mm(pi, rgw[:, 1, c, :], xa)
            yield
            S.act(r_, pr, AF.Sigmoid, bias=col(P_RBR, c))
            S.act(i_, pi, AF.Sigmoid, bias=col(P_RBI, c))
            S.act(a_, r_, AF.Exp, scale=cl[:, c:c + 1])
            S.act(b_, r_, AF.Exp, scale=cl2[:, c:c + 1])
            yield
            S.ts("dve", b_, b_, -1.0, 1.0, ALU.mult, ALU.add)
            S.ts("dve", b_, b_, 1e-20, None, ALU.max)
            S.act(b_, b_, AF.Ln)
            S.act(b_, b_, AF.Exp, scale=0.5)
            S.tt("dve", i_, i_, xa, ALU.mult)
            yield
            S.tt("dve", b_, b_, i_, ALU.mult)
            h = hb[t % 2]
            init = 0.0 if t == 0 else hb[(t - 1) % 2][:, 511:512]
            S.scan(h, a_, b_, init, ALU.mult, ALU.add)
            S.tt("dve", Y[:, c, sl], h, gl, ALU.mult)

        def chunk(c, b, bx, bg):
            for t in range(4):
                yield from unit(c, t, b, bx, bg)

        specs = []
        for c0 in range(0, 8, 2):
            for o in (O_RGX, O_RGG):
                for c in (c0, c0 + 1):
                    specs.append((self.w_in[l, :, o + c * 128:o + (c + 1) * 128], KC))
        for j, c0 in enumerate(range(0, 8, 2)):
            bufs = self.wb
            for i in range(4):
                self.wload(bufs[i][:, 0:KC, :], specs[4 * j + i][0], KC)
            interleave([chunk(c0 + si, streams[si], bufs[si], bufs[2 + si]) for si in range(2)])

    def branch_B(self, l):
        S, PV, Y, CF = self.S, self.PV, self.Y, self.CF
        WP, XH = self.WP, self.XH
        QT = self.bv(WP, 4096).rearrange("p (c t) -> p c t", t=T)
        KT = self.bv(WP + 2048, 4096).rearrange("p (c t) -> p c t", t=T)
        VG = self.bv(WP + 4096, 4096).rearrange("p (b n) -> p b n", n=256)
        DEN = self.fv(WP + 6144, 4096).rearrange("p (m t) -> p m t", t=T)
        WV = self.bv(WP + 10240, 2048).rearrange("p (k n) -> p k n", n=256)
        bank = lambda b: self.ps[:, b * 512:(b + 1) * 512]
        xb = [XH]

        def nb(n):
            v = self.fv(xb[0], n)
            xb[0] += n
            return v
        qs_ = []
        for si in range(2):
            qs_.append({"raw": nb(512), "sq": nb(256).bitcast(BF16), "rs": nb(512), "pp": bank(2 * si), "p2": bank(2 * si + 1)})
        bs_ = []
        for si in range(2):
            bs_.append({"sc": [nb(512), nb(512)], "E": [nb(256).bitcast(BF16), nb(256).bitcast(BF16)],
                        "bk": [bank(4 * si + j) for j in range(4)]})
        assert xb[0] <= XH + 8192
        for g, (win, dil) in enumerate(ATT_GROUPS):
            ci_specs = []
            for which in (O_AQ, O_AK):
                for jc in range(2):
                    c0 = which + g * 256 + jc * 128
                    ci_specs.append(self.w_in[l, :, c0:c0 + 128])
            for i in range(4):
                self.wload(self.wb[i][:, 0:KC, :], ci_specs[i], KC)
            self.wload(WV, self.w_in[l, :, O_AV + g * 256:O_AV + (g + 1) * 256], KC)
            nblk = 16 // dil

            def qk_stream(wi, jc, st):
                dstT, pn, eb = ((QT, P_AQN, float(np.log(0.125))), (KT, P_AKN, 0.0))[wi]
                wbuf = self.wb[wi * 2 + jc]
                gcol = PV[:, pn + 2 * g + jc:pn + 2 * g + jc + 1]
                raw, sq, rs, ps, ps2 = st["raw"], st["sq"], st["rs"], st["pp"], st["p2"]
                for t in range(4):
                    sl = slice(t * 512, (t + 1) * 512)
                    for k in range(KC):
                        S.mm(ps, wbuf[:, k, :], self.uT[:, k, sl], start=(k == 0), stop=(k == KC - 1))
                    yield
                    S.copy("act", raw, ps)
                    S.act(sq, ps, AF.Square)
                    S.mm(ps2, self.bd64_b, sq)
                    yield
                    S.act(rs, ps2, AF.Ln, bias=EPS, scale=1.0 / 64)
                    S.act(rs, rs, AF.Exp, scale=-0.5, bias=eb)
                    n_l = 512 // dil
                    dst = dstT[:, jc, :].rearrange("p (r l) -> p r l", r=dil)[:, :, t * n_l:(t + 1) * n_l]
                    S.stt("dve", dst, raw.rearrange("p (l r) -> p r l", r=dil), gcol,
                          rs.rearrange("p (l r) -> p r l", r=dil), ALU.mult, ALU.mult)

            def v_stream():
                for b in range(16):
                    r, n = divmod(b, nblk)
                    tok = slice(n * 128 * dil + r, n * 128 * dil + r + 127 * dil + 1, dil)
                    ps = bank(4 + b % 2)[:, 0:256]
                    for k in range(KC):
                        S.mm(ps, self.uT[:, k, tok], WV[:, k, :], start=(k == 0), stop=(k == KC - 1))
                    yield
                    S.copy("act", VG[:, b, :], ps)

            interleave([qk_stream(0, 0, qs_[0]), qk_stream(0, 1, qs_[1]), v_stream()])
            interleave([qk_stream(1, 0, qs_[0]), qk_stream(1, 1, qs_[1])])
            Bc = CF[:, C_AB + g * 1024:C_AB + g * 1024 + 512]
            Bp = CF[:, C_AB + g * 1024 + 512:C_AB + g * 1024 + 1024]

            def block(b, st):
                sc, E, bk = st["sc"], st["E"], st["bk"]
                r, n = divmod(b, nblk)
                tok = slice(n * 128 * dil + r, n * 128 * dil + r + 127 * dil + 1, dil)
                qs = slice(b * 128, (b + 1) * 128)
                ks = slice((b - 1) * 128, b * 128)
                for ei, (kk, Bt) in enumerate(((qs, Bc), (ks, Bp))):
                    if ei == 1 and n == 0:
                        continue
                    pcs = [bk[2 * ei], bk[2 * ei + 1]]
                    for j in range(4):
                        jc, hh = divmod(j, 2)
                        S.mm(pcs[hh][:, jc * 128:(jc + 1) * 128], KT[64 * hh:64 * hh + 64, jc, kk],
                             QT[64 * hh:64 * hh + 64, jc, qs])
                    yield
                    for hh in range(2):
                        S.tt("dve", sc[ei].rearrange("p (m h q) -> p h m q", m=2, h=2)[:, hh],
                             pcs[hh][:, 0:256].rearrange("p (m q) -> p m q", m=2),
                             Bt.rearrange("p (m h q) -> p h m q", m=2, h=2)[:, hh], ALU.add)
                    S.act(E[ei], sc[ei], AF.Exp)
                yield
                po, pd = bk[0], bk[1]
                for j in range(4):
                    m, hh = divmod(j, 2)
                    o_out = po[64 * hh:64 * hh + 64, m * 128:(m + 1) * 128]
                    d_out = pd[64 * hh:64 * hh + 64, m * 128:(m + 1) * 128]
                    ej = slice(j * 128, (j + 1) * 128)
                    if n > 0:
                        S.mm(o_out, VG[:, b - 1, j * 64:(j + 1) * 64], E[1][:, ej], start=True, stop=False)
                    S.mm(o_out, VG[:, b, j * 64:(j + 1) * 64], E[0][:, ej], start=(n == 0), stop=True)
                    if n > 0:
                        S.mm(d_out, self.ones_b[:, 0:64], E[1][:, ej], start=True, stop=False)
                    S.mm(d_out, self.ones_b[:, 0:64], E[0][:, ej], start=(n == 0), stop=True)
                yield
                po3 = po[:, 0:256].rearrange("p (m q) -> p m q", m=2)
                pd3 = pd[:, 0:256].rearrange("p (m q) -> p m q", m=2)
                S.copy("act", Y[:, 2 * g:2 * g + 2, tok], po3)
                if g == 0:
                    S.copy("dve", DEN[:, :, tok], pd3)
                else:
                    S.tt("dve", DEN[:, :, tok], DEN[:, :, tok], pd3, ALU.add)

            for b in range(0, 16, 2):
                interleave([block(b, bs_[0]), block(b + 1, bs_[1])])
        for m in range(2):
            S.recip(DEN[:, m, :], DEN[:, m, :])
        for c in range(6):
            S.tt("dve", Y[:, c, :], Y[:, c, :], DEN[:, c % 2, :], ALU.mult)

    def branch_C(self, l, NH=2):
        S, PV, Y, CF = self.S, self.PV, self.Y, self.CF
        W, XH = A_WORK, self.XH
        ident = self.ident
        UT = CF[:, C_UT:C_UT + 128]
        self.wb = [self.bv(W + i * 512, 1024).rearrange("p (k n) -> p k n", n=128) for i in range(4)]
        wbase = [W + 2048]

        def wal(n):
            v = self.fv(wbase[0], n)
            wbase[0] += n
            return v
        slots = []
        for s_ in range(2):
            d = {}
            d["kT"], d["qT"] = wal(2048), wal(2048)
            d["Ktok"] = wal(2048).rearrange("p (n d) -> p n d", d=128)
            d["Vtok"] = wal(1024).bitcast(BF16).rearrange("p (n d) -> p n d", d=128)
            d["S2"] = [wal(128), wal(128)]
            d["sqo"] = wal(64).bitcast(BF16)
            slots.append(d)
        BGf = wal(256)
        BG = BGf.rearrange("p (n c) -> p n c", c=16)
        GAM = wal(128).rearrange("p (n c) -> p n c", c=8)
        NB = wal(128).rearrange("p (n c) -> p n c", c=8)

        def mk_cs(al):
            c = {"xin": [al(520)[:, 0:515] for _ in range(2)]}
            c["cbuf"], c["sbuf"], c["rs"] = al(512), al(512), al(512)
            c["sq"] = al(256).bitcast(BF16)
            return c
        css = [mk_cs(wal)]
        colsb = wal(32)
        assert wbase[0] <= W + 22016, wbase[0] - W
        cbuf = css[0]["cbuf"]
        der = self.fv(A_DER, 64)
        nea = der[:, 24:32]
        xbase = [XH]

        def nb(n=128):
            v = self.fv(xbase[0], n)
            xbase[0] += n
            return v
        bank = lambda b: self.ps[:, b * 512:b * 512 + 128]
        lanes = []
        for li in range(2):
            ln = {}
            for nm in ("Gbc", "tmpT", "tmp2", "Ebc", "Aneg", "Fneg", "Td", "Wm"):
                ln[nm] = nb()
            ln["DT"], ln["dec"] = ln["tmpT"], ln["tmp2"]
            for nm in ("M", "MT", "XTn"):
                ln[nm] = [nb(), nb()]
            ln["tcol"], ln["kdc"] = colsb[:, 2 * li:2 * li + 1], colsb[:, 2 * li + 1:2 * li + 2]
            ln["L"] = [bank(2 * li), bank(2 * li + 1)]
            lanes.append(ln)
        ring = {nm: [nb(), nb(), nb()] for nm in ("TT", "QKT", "KgT", "QgT", "Kdec", "Vb")}
        ring["cdc"] = [colsb[:, 8 + i:9 + i] for i in range(3)]
        recp = {nm: nb() for nm in ("Rb", "ub", "osb", "rso")}
        assert xbase[0] <= XH + 8192, xbase[0] - XH
        RB = [bank(4), bank(5)]
        HB = [self.ps[:, 6 * 512:7 * 512], self.ps[:, 7 * 512:8 * 512]]

        def ctx(d, n, lane=None):
            dd = dict(d)
            dd.update(ring)
            if lane is not None:
                dd.update(lane)
                L = lane["L"]
            else:
                dd.update(recp)
                L = [None, None]
            dd["pt"] = [L[0], L[1], L[0], L[1], L[0], L[1], L[0], RB[0], RB[1], RB[0], RB[1], RB[0]]
            return dd

        WBA = self.wb[0]
        self.wload(WBA, self.w_in[l, :, O_BETA:O_BETA + 128], KC)
        pbg = self.bank()
        for n in range(16):
            for k in range(KC):
                S.mm(pbg[:, n * 16:(n + 1) * 16], self.uT[:, k, n * 128:(n + 1) * 128], WBA[:, k, 0:16],
                     start=(k == 0), stop=(k == KC - 1))
        pbg3 = pbg[:, 0:256].rearrange("p (n c) -> p n c", c=16)
        S.act(nea, PV[:, P_ALOG:P_ALOG + 8], AF.Exp)
        S.ts("dve", nea, nea, -1.0, None, ALU.mult)
        xg = cbuf[:, 0:256]
        xg3 = xg.rearrange("p (n c) -> p n c", c=16)
        S.act(BGf, pbg[:, 0:256], AF.Sigmoid)
        S.memset("dve", xg, 1.0)
        for h in range(8):
            S.act(xg3[:, :, 8 + h], pbg3[:, :, 8 + h], AF.Exp, bias=PV[:, P_DTB + h:P_DTB + h + 1])
        S.act(xg, xg, AF.Ln, bias=1.0)
        for h in range(8):
            S.ts("dve", BG[:, :, 8 + h], xg3[:, :, 8 + h], nea[:, h:h + 1], None, ALU.mult)
        for h in range(8):
            S.ts("dve", NB[:, :, h], BG[:, :, h], -1.0, None, ALU.mult)
        pgam = self.bank()
        S.mm(pgam[:, 0:256], UT, BGf)
        pgam3 = pgam[:, 0:256].rearrange("p (n c) -> p n c", c=16)
        for h in range(8):
            S.copy("act", GAM[:, :, h], pgam3[:, :, 8 + h])

        col = lambda j, k: PV[:, P_DCW + k * 24 + j:P_DCW + k * 24 + j + 1]

        def convsilu(ps, j, t, cs):
            xin, cb = cs["xin"], cs["cbuf"]
            xi = xin[t % 2]
            if t == 0:
                S.memset("pool", xi[:, 0:3], 0.0)
            else:
                S.copy("pool", xi[:, 0:3], xin[(t - 1) % 2][:, 512:515])
            S.copy("act", xi[:, 3:515], ps)
            S.ts("dve", cb, xi[:, 3:515], col(j, 3), None, ALU.mult)
            for k in (2, 1, 0):
                if k == 1:
                    yield
                S.stt("dve", cb, xi[:, k:k + 512], col(j, k), cb, ALU.mult, ALU.add)
            yield
            S.act(cs["sbuf"], cb, AF.Silu)

        def head_prep(h, d, get, gbase, i, cs):
            kT, qT, Ktok, Vtok = d["kT"], d["qT"], d["Ktok"], d["Vtok"]
            bk = [HB[0], HB[0], HB[1], HB[1]]

            def proj_to(ps, wbuf, sl):
                for k in range(KC):
                    S.mm(ps, wbuf[:, k, :], self.uT[:, k, sl], start=(k == 0), stop=(k == KC - 1))
                    if k % 2 == 1 and k < KC - 1:
                        yield
            wz = get(gbase + i)
            for t in range(4):
                sl = slice(t * 512, (t + 1) * 512)
                yield from proj_to(bk[t % 2], wz, sl)
                yield
                S.act(Y[:, h, sl], bk[t % 2], AF.Silu)
            lnq = float(np.log(128.0 ** -0.5))
            for stage, (jofs, dst, eb, tok3) in enumerate(((0, qT, lnq, None), (8, kT, 0.0, Ktok), (16, None, None, Vtok))):
                wbuf = get(gbase + (stage + 1) + i)
                for t in range(4):
                    sl = slice(t * 512, (t + 1) * 512)
                    ps = bk[t % 2]
                    yield from proj_to(ps, wbuf, sl)
                    yield
                    yield from convsilu(ps, jofs + h, t, cs)
                    yield
                    if dst is not None:
                        S.act(cs["sq"], cs["sbuf"], AF.Square)
                        S.mm(bk[2], self.ones_b, cs["sq"])
                        yield
                        S.act(cs["rs"], bk[2], AF.Ln, bias=EPS)
                        S.act(cs["rs"], cs["rs"], AF.Exp, scale=-0.5, bias=eb)
                        S.tt("dve", dst[:, sl], cs["sbuf"], cs["rs"], ALU.mult)
                        src = dst[:, sl]
                    else:
                        src = cs["sbuf"]
                    if tok3 is not None:
                        for ii in range(4):
                            S.tr(bk[3][:, ii * 128:(ii + 1) * 128], src[:, ii * 128:(ii + 1) * 128], ident)
                        yield
                        S.copy("act", tok3[:, 4 * t:4 * t + 4, :], bk[3].rearrange("p (n d) -> p n d", d=128))

        def prep(h, n, d):
            par = n % 3
            sl = slice(n * 128, (n + 1) * 128)
            kT, qT, Ktok, Vtok = d["kT"], d["qT"], d["Ktok"], d["Vtok"]
            M, MT, XTn = d["M"], d["MT"], d["XTn"]
            gc, bc = BG[:, n, 8 + h:9 + h], BG[:, n, h:h + 1]
            nbc, gam = NB[:, n, h:h + 1], GAM[:, n, h:h + 1]
            S.ts("dve", d["Gbc"], CF[:, C_ONES:C_ONES + 128], gc, None, ALU.mult)
            pG = d["pt"][0]
            S.mm(pG, d["Gbc"], UT)
            pKK = d["pt"][1]
            S.mm(pKK, kT[:, sl], kT[:, sl])
            yield
            S.stt("dve", d["tmpT"], pG, gam, CF[:, C_NEGUT:C_NEGUT + 128], ALU.subtract, ALU.add)
            S.act(d["DT"], d["tmpT"], AF.Exp)
            S.stt("dve", d["tmp2"], pG, gam, CF[:, C_NEGLT:C_NEGLT + 128], ALU.subtract, ALU.subtract)
            S.act(d["dec"], d["tmp2"], AF.Exp, scale=-1.0)
            S.act(d["Ebc"], pG, AF.Exp)
            S.tt("dve", d["tcol"], pG[:, 127:128], gam, ALU.subtract)
            S.act(d["kdc"], d["tcol"], AF.Exp)
            S.act(d["cdc"][par], pG[:, 127:128], AF.Exp)
            pKQ = d["pt"][2]
            S.mm(pKQ, kT[:, sl], qT[:, sl])
            yield
            S.stt("dve", d["Aneg"], pKK, nbc, d["dec"], ALU.mult, ALU.mult)
            S.tt("pool", M[0], d["Aneg"], CF[:, C_BD:C_BD + 128], ALU.mult)
            S.tt("pool", d["Fneg"], d["Aneg"], CF[:, C_OFF:C_OFF + 128], ALU.mult)
            S.tt("dve", d["QKT"][par], pKQ, d["DT"], ALU.mult)
            S.tt("pool", d["KgT"][par], kT[:, sl], d["Ebc"], ALU.mult)
            S.tt("pool", d["QgT"][par], qT[:, sl], d["Ebc"], ALU.mult)
            S.ts("pool", d["Kdec"][par], Ktok[:, n, :], d["kdc"], None, ALU.mult)
            S.ts("pool", d["Vb"][par], Vtok[:, n, :], bc, None, ALU.mult)
            pT = d["pt"][3]
            S.tr(pT, M[0], ident)
            yield
            S.copy("act", MT[0], pT)
            S.tt("dve", XTn[0], MT[0], ident, ALU.add)
            for k in range(5):
                a, b = k % 2, (k + 1) % 2
                pM = d["pt"][4]
                S.mm(pM, MT[a], M[a])
                if k < 4:
                    pMT = d["pt"][5]
                    S.mm(pMT, M[a], MT[a])
                yield
                S.copy("act", M[b], pM)
                if k < 4:
                    S.copy("dve", MT[b], pMT)
                pX = d["pt"][6]
                S.mm(pX, M[b], XTn[a])
                yield
                S.tt("dve", XTn[b], XTn[a], pX, ALU.add)
            TdT = XTn[1]
            pTd = d["pt"][3]
            S.tr(pTd, TdT, ident)
            pW = d["pt"][4]
            S.mm(pW, d["Fneg"], TdT)
            yield
            S.copy("act", d["Td"], pTd)
            S.copy("dve", d["Wm"], pW)
            pP = d["pt"][5]
            S.mm(pP, d["Td"], d["Wm"])
            yield
            S.tt("dve", d["TT"][par], TdT, pP, ALU.add)

        def rec(h, n, d):
            par = n % 3
            sl = slice(n * 128, (n + 1) * 128)
            nbc = NB[:, n, h:h + 1]
            Sold, Snew = d["S2"][n % 2], d["S2"][(n + 1) % 2]
            Rb, ub, osb, rso, sqo = d["Rb"], d["ub"], d["osb"], d["rso"], d["sqo"]
            if n > 0:
                pKS = d["pt"][7]
                S.mm(pKS, d["KgT"][par], Sold)
                yield
                S.stt("dve", Rb, pKS, nbc, d["Vb"][par], ALU.mult, ALU.add)
                Rv = Rb
            else:
                Rv = d["Vb"][par]
            pu = d["pt"][8]
            S.mm(pu, d["TT"][par], Rv)
            yield
            S.copy("act", ub, pu)
            po = d["pt"][9]
            if n > 0:
                S.mm(po, Sold, d["QgT"][par], start=True, stop=False)
            S.mm(po, ub, d["QKT"][par], start=(n == 0), stop=True)
            pS = d["pt"][10]
            S.mm(pS, d["Kdec"][par], ub)
            yield
            if n > 0:
                S.ts("dve", Snew, Sold, d["cdc"][par], None, ALU.mult)
                S.tt("dve", Snew, pS, Snew, ALU.add)
            else:
                S.copy("dve", Snew, pS)
            S.act(sqo, po, AF.Square)
            S.copy("act", osb, po)
            ps2 = d["pt"][11]
            S.mm(ps2, self.ones_b, sqo)
            yield
            S.act(rso, ps2, AF.Ln, bias=EPS, scale=1.0 / 128)
            S.act(rso, rso, AF.Exp, scale=-0.5)
            S.stt("dve", osb, osb, PV[:, P_DON + h:P_DON + h + 1], rso, ALU.mult, ALU.mult)
            S.tt("dve", Y[:, h, sl], Y[:, h, sl], osb, ALU.mult)

        specs = []
        for h in range(8):
            for o in (O_DZ, O_DQ, O_DK, O_DV):
                specs.append((self.w_in[l, :, o + h * 128:o + (h + 1) * 128], KC))
        get = self.stream(specs, self.wb)

        def run_head(h, d, extra):
            active = {}
            if extra is not None:
                active["hp"] = extra
            fin_prep, fin_rec = set(), set()
            st = {"np": 0, "nr": 0, "rec_busy": False, "lane": [False, False]}
            while True:
                n = st["np"]
                if n < 16 and not st["lane"][n % 2] and (n < 3 or (n - 3) in fin_rec):
                    active[("p", n)] = prep(h, n, ctx(d, n, lanes[n % 2]))
                    st["lane"][n % 2] = True
                    st["np"] += 1
                    continue
                if not st["rec_busy"] and st["nr"] < 16 and st["nr"] in fin_prep:
                    active[("r", st["nr"])] = rec(h, st["nr"], ctx(d, st["nr"]))
                    st["rec_busy"] = True
                if not active:
                    break
                prio = lambda k_: 2 if k_ == "hp" else (0 if k_[0] == "r" else 1)
                for key in sorted(active, key=lambda k_: (prio(k_), 0 if k_ == "hp" else k_[1])):
                    try:
                        next(active[key])
                    except StopIteration:
                        del active[key]
                        if key == "hp":
                            pass
                        elif key[0] == "p":
                            fin_prep.add(key[1])
                            st["lane"][key[1] % 2] = False
                        else:
                            fin_rec.add(key[1])
                            st["rec_busy"] = False
                            st["nr"] += 1

        interleave([head_prep(0, slots[0], get, 0, 0, css[0])])
        for h in range(8):
            nxt = head_prep(h + 1, slots[(h + 1) % 2], get, 4 * (h + 1), 0, css[0]) if h + 1 < 8 else None
            run_head(h, slots[h % 2], nxt)


def host_tables():
    cf = np.zeros((128, NCF), np.float32)
    i = np.arange(128)
    cf[:, C_ID:C_ID + 128] = np.eye(128, dtype=np.float32)
    P, Fr = np.meshgrid(i, i, indexing="ij")
    cf[:, C_NEGUT:C_NEGUT + 128] = np.where(Fr >= P, 0.0, NEG)
    cf[:, C_NEGLT:C_NEGLT + 128] = np.where(Fr < P, 0.0, NEG)
    cf[:, C_BD:C_BD + 128] = ((P // 64) == (Fr // 64)).astype(np.float32)
    cf[:, C_OFF:C_OFF + 128] = ((P >= 64) & (Fr < 64)).astype(np.float32)
    cf[:, C_UT:C_UT + 128] = (P <= Fr).astype(np.float32)
    cf[:, C_ONES:C_ONES + 128] = 1.0
    h = np.arange(1, 13, dtype=np.float32)
    slopes = np.exp2(np.float32(-8.0) * h / np.float32(12)).astype(np.float32)
    for g, (win, dil) in enumerate(ATT_GROUPS):
        for j in range(4):
            sl = slopes[g * 4 + j]
            k_, q_ = P, Fr
            dcur = (q_ - k_).astype(np.float32)
            cur = np.where(k_ <= q_, -sl * (dcur * dil), NEG)
            dprev = (q_ + 128 - k_).astype(np.float32)
            prev = np.where(k_ >= q_, -sl * (dprev * dil), NEG)
            base = C_AB + g * 1024
            cf[:, base + j * 128: base + (j + 1) * 128] = cur
            cf[:, base + 512 + j * 128: base + 512 + (j + 1) * 128] = prev
    cb = np.zeros((128, 256), np.float32)
    cb[:, 0:128] = 1.0
    cb[:, 128:256] = ((P // 64) == (Fr // 64)).astype(np.float32)
    return cf, cb


def host_params(inp):
    pv = np.zeros((L, 128, NPV), np.float32)
    col = lambda v: np.ascontiguousarray(np.asarray(v, np.float32).reshape(-1, 128).T)
    for l in range(L):
        pv[l, :, P_N1:P_N1 + 8] = col(inp["ffn1_norm"][l])
        pv[l, :, P_NM:P_NM + 8] = col(inp["mix_norm"][l])
        pv[l, :, P_N2:P_N2 + 8] = col(inp["ffn2_norm"][l])
        for k in range(4):
            pv[l, :, P_RCW + k * 8:P_RCW + (k + 1) * 8] = col(inp["rg_conv_w"][l, k])
            pv[l, :, P_DCW + k * 24:P_DCW + (k + 1) * 24] = col(inp["dn_conv_w"][l, k])
        pv[l, :, P_RCB:P_RCB + 8] = col(inp["rg_conv_b"][l])
        pv[l, :, P_RBR:P_RBR + 8] = col(inp["rg_b_r"][l])
        pv[l, :, P_RBI:P_RBI + 8] = col(inp["rg_b_i"][l])
        pv[l, :, P_LAM:P_LAM + 8] = col(inp["rg_lambda"][l])
        pv[l, :, P_AQN:P_AQN + 6] = col(inp["att_q_norm"][l])
        pv[l, :, P_AKN:P_AKN + 6] = col(inp["att_k_norm"][l])
        pv[l, :, P_DON:P_DON + 8] = col(inp["dn_out_norm"][l])
        pv[l, :, P_ALOG:P_ALOG + 8] = np.broadcast_to(np.asarray(inp["dn_a_log"][l], np.float32)[None, :], (128, 8))
        pv[l, :, P_DTB:P_DTB + 8] = np.broadcast_to(np.asarray(inp["dn_dt_bias"][l], np.float32)[None, :], (128, 8))
    rgw = np.zeros((L, 128, 2, 8, 128), np.float32)
    for l in range(L):
        for gi, nm in enumerate(("rg_w_r", "rg_w_i")):
            w = np.asarray(inp[nm][l], np.float32)
            for c in range(8):
                rgw[l, 0:64, gi, c, 0:64] = w[2 * c]
                rgw[l, 64:128, gi, c, 64:128] = w[2 * c + 1]
    return pv, rgw.reshape(L, 128, 2048)


_CACHE = {}


def run(inputs, nlayers=L, stop_after=None, ncores=8, dbgs=()):
    key = (nlayers, stop_after, tuple(dbgs))
    if key not in _CACHE:
        b_ = Builder(nlayers, stop_after)
        b_.dbgs = set(dbgs)
        _CACHE[key] = (b_.build(), b_)
    nc, b_ = _CACHE[key]
    cf, cb = host_tables()
    pv, rgw = host_params(inputs)
    x = np.asarray(inputs["x"], np.float32)
    shared = {"cf": cf, "cb": cb, "pv": pv, "rgw": rgw}
    for f in ("ffn1", "ffn2"):
        for s in ("w_gate", "w_up", "w_down"):
            shared[f + "_" + s] = np.ascontiguousarray(np.asarray(inputs[f + "_" + s], np.float32))
    for nm in ("w_in", "w_branch", "w_out"):
        shared[nm] = np.ascontiguousarray(np.asarray(inputs[nm], np.float32))
    in_maps = []
    for b in range(ncores):
        m = dict(shared)
        m["xT"] = np.ascontiguousarray(x[b].T)
        in_maps.append(m)
    res = run_bass_kernel_spmd(nc, in_maps, core_ids=list(range(ncores)))
    if b_.dbg_names:
        run.dbg = {n: res.results[0][n] for n in b_.dbg_names}
    return np.stack([np.ascontiguousarray(r["y"].T) for r in res.results], axis=0)


def kernel(**inputs):
    return run(inputs).astype(np.float32)
```

```python
import contextlib
import os as _os
import numpy as np
import concourse.bass as bass
import concourse.mybir as mybir
from concourse.bass_utils import run_bass_kernel_spmd

F32 = mybir.dt.float32
BF16 = mybir.dt.bfloat16
AF = mybir.ActivationFunctionType
ALU = mybir.AluOpType
ESZ = {F32: 4, BF16: 2}

T = 2048
D = 1024
FF = 2816
L = 2
KC = 8
IN_DIM = 11536
O_RGX, O_RGG, O_AQ, O_AK, O_AV = 0, 1024, 2048, 2816, 3584
O_DQ, O_DK, O_DV, O_DZ, O_BETA, O_ALPHA, O_MG = 4352, 5376, 6400, 7424, 8448, 8456, 8464
EPS = 1e-6
NEG = -30000.0
ATT_GROUPS = ((128, 1), (512, 4), (2048, 16))

P_N1, P_NM, P_N2, P_RCW, P_RCB, P_RBR, P_RBI, P_LAM = 0, 8, 16, 24, 56, 64, 72, 80
P_AQN, P_AKN, P_DCW, P_DON, P_ALOG, P_DTB = 88, 94, 100, 196, 204, 212
NPV = 224
C_ID, C_NEGUT, C_NEGLT, C_BD, C_OFF, C_UT, C_ONES, C_AB = 0, 128, 256, 384, 512, 640, 768, 896
NCF = 896 + 3072
A_XT, A_UT, A_CF, A_CB, A_PV, A_DER, A_WORK, A_END = 0, 16384, 24576, 28544, 28672, 29120, 29184, 51200


def _box(ap):
    es = ESZ[ap.dtype]
    pat = ap.ap
    space = str(ap.space)
    if space == "DRAM":
        return ("DRAM", ap.tensor.name, 0, 1, 0, 1)
    pstep, pcnt = pat[0]
    if pstep == 0:
        row = int(np.prod(ap.tensor.shape[1:]))
        pcnt = 1
    else:
        row = pstep
    p0 = ap.offset // row
    c0 = ap.offset % row
    ext = 1
    for st, cnt in pat[1:]:
        ext += (cnt - 1) * abs(st)
    return (space, ap.tensor.name, p0, p0 + pcnt, c0 * es, (c0 + ext) * es)


class Sched:
    ENGS = ("pe", "act", "dve", "pool", "sp")
    BUCKET = 256
    EPOCH = 4000
    NDMA = 24

    def __init__(self, nc):
        self.nc = nc
        self.ops = {e: [] for e in self.ENGS}
        self.tab = {}
        self.ndma = {e: 0 for e in self.ENGS}

    def _keys(self, box):
        space, name, p0, p1, b0, b1 = box
        if space == "DRAM":
            return [(space, name, 0, 0)]
        if space == "PSUM":
            return [(space, name, 0, b) for b in range(b0 // 2048, (b1 - 1) // 2048 + 1)]
        ks = []
        for q in range(p0 // 32, (p1 - 1) // 32 + 1):
            for b in range(b0 // self.BUCKET, (b1 - 1) // self.BUCKET + 1):
                ks.append((space, name, q, b))
        return ks

    def op(self, eng, fn, reads, writes, dma=False):
        idx = len(self.ops[eng])
        if dma:
            tok = ("dma", eng, self.ndma[eng])
            self.ndma[eng] += 1
        else:
            tok = (eng, idx)
        deps = set()
        rk, wk = [], []
        for ap in reads:
            rk += self._keys(_box(ap))
        for ap in writes:
            wk += self._keys(_box(ap))
        for k in rk:
            ent = self.tab.get(k)
            if ent is not None and ent[0] is not None:
                deps.add(ent[0])
        for k in wk:
            ent = self.tab.get(k)
            if ent is not None:
                if ent[0] is not None:
                    deps.add(ent[0])
                deps.update(ent[1].values())
        for k in rk:
            ent = self.tab.setdefault(k, [None, {}])
            ent[1][tok if dma else tok[:-1]] = tok
        for k in wk:
            self.tab[k] = [tok, {}]
        deps.discard(tok)
        self.ops[eng].append({"fn": fn, "deps": deps, "dma": tok if dma else None})
        return tok

    def mm(self, out, lhsT, rhs, start=True, stop=True):
        return self.op("pe", lambda e: e.matmul(out, lhsT, rhs, start=start, stop=stop),
                       [lhsT, rhs] + ([] if start else [out]), [out])

    def tr(self, out, in_, ident):
        return self.op("pe", lambda e: e.transpose(out, in_, ident), [in_, ident], [out])

    def act(self, out, in_, func, bias=None, scale=None):
        kw = {}
        rd = [in_]
        if bias is not None:
            kw["bias"] = bias
            if not isinstance(bias, (int, float)):
                rd.append(bias)
        if scale is not None:
            kw["scale"] = scale
            if not isinstance(scale, (int, float)):
                rd.append(scale)
        return self.op("act", lambda e: e.activation(out, in_, func, **kw), rd, [out])

    def tt(self, eng, out, in0, in1, op):
        return self.op(eng, lambda e: e.tensor_tensor(out, in0, in1, op), [in0, in1], [out])

    def ts(self, eng, out, in0, s1, s2, op0, op1=None):
        rd = [in0] + [s for s in (s1, s2) if s is not None and not isinstance(s, (int, float))]
        if op1 is None:
            return self.op(eng, lambda e: e.tensor_scalar(out, in0, s1, None, op0), rd, [out])
        return self.op(eng, lambda e: e.tensor_scalar(out, in0, s1, s2, op0, op1), rd, [out])

    def stt(self, eng, out, in0, scalar, in1, op0, op1):
        rd = [in0, in1] + ([] if isinstance(scalar, (int, float)) else [scalar])
        return self.op(eng, lambda e: e.scalar_tensor_tensor(out, in0, scalar, in1, op0, op1), rd, [out])

    def copy(self, eng, out, in_):
        if eng == "act":
            return self.op(eng, lambda e: e.copy(out, in_), [in_], [out])
        return self.op(eng, lambda e: e.tensor_copy(out, in_), [in_], [out])

    def scan(self, out, d0, d1, init, op0, op1):
        rd = [d0, d1] + ([] if isinstance(init, (int, float)) else [init])
        return self.op("dve", lambda e: e.tensor_tensor_scan(out, d0, d1, init, op0, op1), rd, [out])

    def recip(self, out, in_):
        return self.op("dve", lambda e: e.reciprocal(out, in_), [in_], [out])

    def memset(self, eng, ap, val):
        return self.op(eng, lambda e: e.memset(ap, val), [], [ap])

    def dma(self, q, out, in_):
        return self.op(q, lambda e: e.dma_start(out=out, in_=in_), [in_], [out], dma=True)

    def emit(self):
        nc = self.nc
        engs = self.ENGS
        sig = {e: set() for e in engs}
        for f in engs:
            for rec in self.ops[f]:
                for d in rec["deps"]:
                    if d[0] == "dma":
                        continue
                    e, i = d
                    if e == f and e == "pe":
                        continue
                    sig[e].add(i)
        cnt = {e: {} for e in engs}
        for e in engs:
            for c, i in enumerate(sorted(sig[e])):
                cnt[e][i] = c + 1
        nep = {e: (len(cnt[e]) + self.EPOCH - 1) // self.EPOCH for e in engs}
        for e in engs:
            for i, rec in enumerate(self.ops[e]):
                rec["idx"] = i
        with contextlib.ExitStack() as st:
            esem = {e: [st.enter_context(nc.semaphore(f"s_{e}_{k}")) for k in range(nep[e])] for e in engs}
            dsem = {e: [st.enter_context(nc.semaphore(f"d_{e}_{k}")) for k in range(min(self.NDMA, self.ndma[e]))]
                    for e in engs}
            block = st.enter_context(nc.Block())

            def gen(f):
                def body(eobj):
                    waited = {}
                    for rec in self.ops[f]:
                        need = {}
                        for d in rec["deps"]:
                            if d[0] == "dma":
                                _, q, k = d
                                key = ("d", q, k % self.NDMA)
                                val = 16 * (k // self.NDMA + 1)
                            else:
                                e, i = d
                                if e == f and e == "pe":
                                    continue
                                c = cnt[e][i]
                                key = ("e", e, (c - 1) // self.EPOCH)
                                val = (c - 1) % self.EPOCH + 1
                            if need.get(key, 0) < val:
                                need[key] = val
                        if rec["dma"] is not None:
                            _, q, k = rec["dma"]
                            if k >= self.NDMA:
                                key = ("d", q, k % self.NDMA)
                                val = 16 * (k // self.NDMA)
                                if need.get(key, 0) < val:
                                    need[key] = val
                        for key, val in need.items():
                            if waited.get(key, 0) >= val:
                                continue
                            waited[key] = val
                            sem = dsem[key[1]][key[2]] if key[0] == "d" else esem[key[1]][key[2]]
                            eobj.wait_ge(sem, val)
                        ins = rec["fn"](eobj)
                        if rec["dma"] is not None:
                            _, q, k = rec["dma"]
                            ins.then_inc(dsem[q][k % self.NDMA], 16)
                        elif rec["idx"] in cnt[f]:
                            c = cnt[f][rec["idx"]]
                            ins.then_inc(esem[f][(c - 1) // self.EPOCH], 1)
                    if f == "sp":
                        for q in engs:
                            for s in range(min(self.NDMA, self.ndma[q])):
                                n = (self.ndma[q] - 1 - s) // self.NDMA + 1
                                eobj.wait_ge(dsem[q][s], 16 * n)
                return body

            block.tensor(gen("pe"))
            block.scalar(gen("act"))
            block.vector(gen("dve"))
            block.gpsimd(gen("pool"))
            block.sync(gen("sp"))


def interleave(gens):
    gens = list(gens)
    while gens:
        for g in list(gens):
            try:
                next(g)
            except StopIteration:
                gens.remove(g)


class Builder:
    def __init__(self, nlayers=L, stop_after=None):
        self.nlayers = nlayers
        self.stop_after = stop_after
        nc = bass.Bass("TRN2", target_bir_lowering=False)
        self.nc = nc
        dt = lambda name, shape, kind="ExternalInput": nc.dram_tensor(name, shape, F32, kind=kind).ap()
        self.xin = dt("xT", [D, T])
        self.cf = dt("cf", [128, NCF])
        self.cb = dt("cb", [128, 256])
        self.pv = dt("pv", [L, 128, NPV])
        self.rgw = dt("rgw", [L, 128, 2048])
        self.w = {}
        for f in ("ffn1", "ffn2"):
            self.w[f + "_g"] = dt(f + "_w_gate", [L, D, FF])
            self.w[f + "_u"] = dt(f + "_w_up", [L, D, FF])
            self.w[f + "_d"] = dt(f + "_w_down", [L, FF, D])
        self.w_in = dt("w_in", [L, D, IN_DIM])
        self.w_br = dt("w_branch", [L, FF, D])
        self.w_out = dt("w_out", [L, D, D])
        self.xsp = dt("xspill", [D, T], kind="Internal")
        self.yout = dt("y", [D, T], kind="ExternalOutput")
        self.psb = 0
        self.dbg_names = []
        self.dbgs = set()

    def fv(self, c0, n):
        return self.ar[:, c0:c0 + n]

    def bv(self, c0, nbf):
        return self.ar[:, c0:c0 + nbf // 2].bitcast(BF16)

    def bank(self):
        b = self.psb
        self.psb = (self.psb + 1) % 8
        return self.ps[:, b * 512:(b + 1) * 512]

    def build(self):
        nc = self.nc
        with (nc.sbuf_tensor("arena", [128, A_END], F32) as ar, nc.psum_tensor("psum", [128, 4096], F32) as ps):
            self.ar = ar
            self.ps = ps
            self.S = Sched(nc)
            self.program()
            self.S.emit()
        return nc

    def program(self):
        S = self.S
        self.xT = self.fv(A_XT, 16384).rearrange("p (k t) -> p k t", t=T)
        self.uT = self.bv(A_UT, 16384).rearrange("p (k t) -> p k t", t=T)
        self.CF = self.fv(A_CF, NCF)
        self.CB = self.bv(A_CB, 256)
        self.ident = self.CF[:, C_ID:C_ID + 128]
        self.ones_b = self.CB[:, 0:128]
        self.bd64_b = self.CB[:, 128:256]
        S.dma("sp", self.CF, self.cf[:, :])
        S.dma("pool", self.CB, self.cb[:, :])
        for l in range(L):
            S.dma("sp", self.fv(A_PV + l * NPV, NPV), self.pv[l, :, :])
        for k in range(KC):
            S.dma("sp", self.xT[:, k, :], self.xin[k * 128:(k + 1) * 128, :])
        done = False
        for l in range(self.nlayers):
            self.PV = self.fv(A_PV + l * NPV, NPV)
            if not (self.stop_after and self.stop_after[1].endswith("_only")):
                self.ffn(l, "ffn1", P_N1)
            if self.stop_after == (l, "ffn1"):
                done = True
                break
            self.mixer(l)
            if self.stop_after is not None and self.stop_after[0] == l and self.stop_after[1].startswith("mix"):
                done = True
                break
            last = (l == self.nlayers - 1)
            self.ffn(l, "ffn2", P_N2, store=last)
            if last:
                return
        for k in range(KC):
            S.dma("sp", self.yout[k * 128:(k + 1) * 128, :], self.xT[:, k, :])

    def rmsnorm(self, pcol, wbase):
        S = self.S
        for t in range(4):
            sl = slice(t * 512, (t + 1) * 512)
            ps = self.bank()
            R = self.fv(wbase + 1024 + (t % 2) * 512, 512)
            for k in range(KC):
                sq = self.bv(wbase + (k % 4) * 256, 512)
                S.act(sq, self.xT[:, k, sl], AF.Square)
                S.mm(ps, self.ones_b, sq, start=(k == 0), stop=(k == KC - 1))
            S.act(R, ps, AF.Ln, bias=EPS, scale=1.0 / D)
            S.act(R, R, AF.Exp, scale=-0.5)
            for k in range(KC):
                S.stt("dve", self.uT[:, k, sl], self.xT[:, k, sl], self.PV[:, pcol + k:pcol + k + 1], R,
                      ALU.mult, ALU.mult)

    def wload(self, dst, src2d, nk):
        self.S.dma("pool", dst, src2d.rearrange("(k p) n -> p k n", p=128))

    def ffn(self, l, name, pcol, store=False):
        S = self.S
        W = A_WORK
        hT = self.bv(W, 22 * 1024).rearrange("p (f t) -> p f t", t=1024)
        wgu = [self.bv(W + 11264 + i * 1024, 2048).rearrange("p (k n) -> p k n", n=256) for i in range(4)]
        wd = [self.bv(W + 15360 + i * 1408, 2816).rearrange("p (f n) -> p f n", n=128) for i in range(2)]
        sg = [self.fv(W + 18176 + i * 512, 512) for i in range(2)]
        scr = W + 19200
        self.rmsnorm(pcol, scr)
        wg_d, wu_d, wd_d = self.w[name + "_g"], self.w[name + "_u"], self.w[name + "_d"]
        NFB = FF // 256
        for half in range(2):
            t0 = half * 1024
            for fb in range(NFB):
                bg, bu = wgu[(fb % 2) * 2], wgu[(fb % 2) * 2 + 1]
                self.wload(bg, wg_d[l, :, fb * 256:(fb + 1) * 256], KC)
                self.wload(bu, wu_d[l, :, fb * 256:(fb + 1) * 256], KC)
                for fc in range(2):
                    for tt in range(2):
                        sl = slice(t0 + tt * 512, t0 + (tt + 1) * 512)
                        pg, pu = self.bank(), self.bank()
                        for k in range(KC):
                            S.mm(pg, bg[:, k, fc * 128:(fc + 1) * 128], self.uT[:, k, sl], start=(k == 0), stop=(k == KC - 1))
                        for k in range(KC):
                            S.mm(pu, bu[:, k, fc * 128:(fc + 1) * 128], self.uT[:, k, sl], start=(k == 0), stop=(k == KC - 1))
                        s_ = sg[tt]
                        S.act(s_, pg, AF.Silu)
                        S.tt("dve", hT[:, fb * 2 + fc, tt * 512:(tt + 1) * 512], s_, pu, ALU.mult)
            for dc in range(KC):
                b = wd[dc % 2]
                self.wload(b, wd_d[l, :, dc * 128:(dc + 1) * 128], 22)
                for tt in range(2):
                    sl = slice(t0 + tt * 512, t0 + (tt + 1) * 512)
                    p = self.bank()
                    for f in range(22):
                        S.mm(p, b[:, f, :], hT[:, f, tt * 512:(tt + 1) * 512], start=(f == 0), stop=(f == 21))
                    S.stt("dve", self.xT[:, dc, sl], p, 0.5, self.xT[:, dc, sl], ALU.mult, ALU.add)
                if store:
                    S.dma("sp", self.yout[dc * 128:(dc + 1) * 128, t0:t0 + 1024], self.xT[:, dc, t0:t0 + 1024])


    def stream(self, specs, bufs):
        st = {"n": 0}

        def get(i):
            while st["n"] < min(len(specs), i + len(bufs) - 1):
                src, nk = specs[st["n"]]
                self.wload(bufs[st["n"] % len(bufs)][:, 0:nk, :], src, nk)
                st["n"] += 1
            return bufs[i % len(bufs)]
        return get

    def proj(self, wbuf, nk, rhs_of, sl):
        ps = self.bank()
        for k in range(nk):
            self.S.mm(ps, wbuf[:, k, :], rhs_of(k, sl), start=(k == 0), stop=(k == nk - 1))
        return ps

    def uk(self, k, sl):
        return self.uT[:, k, sl]

    def mixer(self, l):
        S = self.S
        W = A_WORK
        self.XH = A_XT + 8192
        self.rmsnorm(P_NM, W + 19200)
        for k in range(KC):
            S.dma("sp", self.xsp[k * 128:(k + 1) * 128, :], self.xT[:, k, :])
        self.Y = self.bv(A_XT, 16384).rearrange("p (k t) -> p k t", t=T)
        self.ACC = self.bv(W, 16384).rearrange("p (k t) -> p k t", t=T)
        WW = W + 8192
        self.wb = [self.bv(WW + i * 512, 1024).rearrange("p (k n) -> p k n", n=128) for i in range(4)]
        self.WP = WW + 2048
        stop = self.stop_after[1] if (self.stop_after and self.stop_after[0] == l) else None
        self.branch_C(l, NH=int(_os.environ.get("NH", "2")))
        if stop in ("mixC", "mixC_only"):
            return self.dump_Y(8)
        self.wb = [self.bv(WW + i * 512, 1024).rearrange("p (k n) -> p k n", n=128) for i in range(4)]
        self.merge(l, 2, 8, 1792, True)
        self.branch_A(l)
        self.merge(l, 0, 8, 0, False)
        self.branch_B(l)
        self.merge(l, 1, 6, 1024, False)
        self.outproj(l)

    def dbg(self, name, ap):
        S = self.S
        p, n = ap.shape
        dr = self.nc.dram_tensor("dbg_" + name, [p, n], F32, kind="ExternalOutput").ap()
        self.dbg_names.append("dbg_" + name)
        stg = self.fv(A_WORK + 21504, 512)
        for c0 in range(0, n, 512):
            c1 = min(n, c0 + 512)
            S.copy("dve", stg[0:p, 0:c1 - c0], ap[:, c0:c1])
            S.dma("sp", dr[:, c0:c1], stg[0:p, 0:c1 - c0])

    def dump_Y(self, n):
        S = self.S
        st = [self.fv(self.WP + i * 2048, 2048) for i in range(2)]
        for c in range(n):
            S.copy("dve", st[c % 2], self.Y[:, c, :])
            S.dma("sp", self.xsp[c * 128:(c + 1) * 128, :], st[c % 2])
        for c in range(KC):
            S.dma("sp", self.xT[:, c, :], self.xsp[c * 128:(c + 1) * 128, :])

    def merge(self, l, b, nkc, row0, first):
        S = self.S
        specs = []
        for dc in range(KC):
            specs.append((self.w_br[l, row0:row0 + nkc * 128, dc * 128:(dc + 1) * 128], nkc))
            specs.append((self.w_in[l, :, O_MG + b * 1024 + dc * 128:O_MG + b * 1024 + (dc + 1) * 128], KC))
        get = self.stream(specs, self.wb)
        sg = [self.fv(self.XH + i * 512, 512) for i in range(2)]
        tmp = [self.fv(self.XH + 1024 + i * 512, 512) for i in range(2)]
        for dc in range(KC):
            wz, wg = get(2 * dc), get(2 * dc + 1)
            for t in range(4):
                sl = slice(t * 512, (t + 1) * 512)
                pz = self.proj(wz, nkc, lambda k, s: self.Y[:, k, s], sl)
                pg = self.proj(wg, KC, self.uk, sl)
                S.act(sg[t % 2], pg, AF.Sigmoid)
                if first:
                    S.tt("dve", self.ACC[:, dc, sl], sg[t % 2], pz, ALU.mult)
                else:
                    S.tt("dve", tmp[t % 2], sg[t % 2], pz, ALU.mult)
                    S.tt("pool", self.ACC[:, dc, sl], self.ACC[:, dc, sl], tmp[t % 2], ALU.add)

    def outproj(self, l):
        S = self.S
        specs = [(self.w_out[l, :, d2 * 128:(d2 + 1) * 128], KC) for d2 in range(KC)]
        get = self.stream(specs, self.wb)
        xold = [self.fv(self.WP + i * 2048, 2048) for i in range(2)]
        S.dma("sp", xold[0], self.xsp[0:128, :])
        for d2 in range(KC):
            if d2 + 1 < KC:
                S.dma("sp", xold[(d2 + 1) % 2], self.xsp[(d2 + 1) * 128:(d2 + 2) * 128, :])
            wo = get(d2)
            for t in range(4):
                sl = slice(t * 512, (t + 1) * 512)
                p = self.proj(wo, KC, lambda k, s: self.ACC[:, k, s], sl)
                S.tt("dve", self.xT[:, d2, sl], p, xold[d2 % 2][:, sl], ALU.add)

    def branch_A(self, l):
        S, PV, Y = self.S, self.PV, self.Y
        rgw = self.fv(self.WP, 2048)
        S.dma("sp", rgw, self.rgw[l, :, :])
        rgw = rgw.rearrange("p (g c n) -> p g c n", g=2, c=8)
        der = self.fv(A_DER, 64)
        e, cl, cl2 = der[:, 0:8], der[:, 8:16], der[:, 16:24]
        S.act(e, PV[:, P_LAM:P_LAM + 8], AF.Exp, scale=-1.0)
        S.act(e, e, AF.Ln, bias=1.0)
        S.ts("dve", cl, e, -8.0, None, ALU.mult)
        S.ts("dve", cl2, e, -16.0, None, ALU.mult)
        col = lambda base_, c: PV[:, base_ + c:base_ + c + 1]
        bases = [self.XH, self.WP + 2048]
        streams = []
        for si in range(2):
            base = [bases[si]]

            def nb(n=512, base=base):
                v = self.fv(base[0], n)
                base[0] += n
                return v
            b = {"xin": [nb(520)[:, 0:515] for _ in range(2)]}
            for nm in ("xa", "r", "i", "a", "b", "gl"):
                b[nm] = nb()
            b["hb"] = [nb(), nb()]
            b["bk"] = [self.ps[:, (4 * si + j) * 512:(4 * si + j + 1) * 512] for j in range(4)]
            streams.append(b)

        def unit(c, t, b, bx, bg):
            sl = slice(t * 512, (t + 1) * 512)
            px, pg, pr, pi = b["bk"]
            xin, xa, r_, i_, a_, b_, gl, hb = b["xin"], b["xa"], b["r"], b["i"], b["a"], b["b"], b["gl"], b["hb"]
            for k in range(KC):
                S.mm(px, bx[:, k, :], self.uT[:, k, sl], start=(k == 0), stop=(k == KC - 1))
            for k in range(KC):
                S.mm(pg, bg[:, k, :], self.uT[:, k, sl], start=(k == 0), stop=(k == KC - 1))
            yield
            xi = xin[t % 2]
            if t == 0:
                S.memset("pool", xi[:, 0:3], 0.0)
            else:
                S.copy("pool", xi[:, 0:3], xin[(t - 1) % 2][:, 512:515])
            S.copy("act", xi[:, 3:515], px)
            S.act(gl, pg, AF.Gelu_apprx_tanh)
            S.ts("dve", xa, xi[:, 3:515], col(P_RCW + 24, c), col(P_RCB, c), ALU.mult, ALU.add)
            for k in (2, 1, 0):
                S.stt("dve", xa, xi[:, k:k + 512], col(P_RCW + 8 * k, c), xa, ALU.mult, ALU.add)
            yield
            S.mm(pr, rgw[:, 0, c, :], xa)
            S.mm(pi, rgw[:, 1, c, :], xa)
            yield
            S.act(r_, pr, AF.Sigmoid, bias=col(P_RBR, c))
            S.act(i_, pi, AF.Sigmoid, bias=col(P_RBI, c))
            S.act(a_, r_, AF.Exp, scale=cl[:, c:c + 1])
            S.act(b_, r_, AF.Exp, scale=cl2[:, c:c + 1])
            yield
            S.ts("dve", b_, b_, -1.0, 1.0, ALU.mult, ALU.add)
            S.ts("dve", b_, b_, 1e-20, None, ALU.max)
            S.act(b_, b_, AF.Ln)
            S.act(b_, b_, AF.Exp, scale=0.5)
            S.tt("dve", i_, i_, xa, ALU.mult)
            yield
            S.tt("dve", b_, b_, i_, ALU.mult)
            h = hb[t % 2]
            init = 0.0 if t == 0 else hb[(t - 1) % 2][:, 511:512]
            S.scan(h, a_, b_, init, ALU.mult, ALU.add)
            S.tt("dve", Y[:, c, sl], h, gl, ALU.mult)

        def chunk(c, b, bx, bg):
            for t in range(4):
                yield from unit(c, t, b, bx, bg)

        specs = []
        for c0 in range(0, 8, 2):
            for o in (O_RGX, O_RGG):
                for c in (c0, c0 + 1):
                    specs.append((self.w_in[l, :, o + c * 128:o + (c + 1) * 128], KC))
        for j, c0 in enumerate(range(0, 8, 2)):
            bufs = self.wb
            for i in range(4):
                self.wload(bufs[i][:, 0:KC, :], specs[4 * j + i][0], KC)
            interleave([chunk(c0 + si, streams[si], bufs[si], bufs[2 + si]) for si in range(2)])

    def branch_B(self, l):
        S, PV, Y, CF = self.S, self.PV, self.Y, self.CF
        WP, XH = self.WP, self.XH
        QT = self.bv(WP, 4096).rearrange("p (c t) -> p c t", t=T)
        KT = self.bv(WP + 2048, 4096).rearrange("p (c t) -> p c t", t=T)
        VG = self.bv(WP + 4096, 4096).rearrange("p (b n) -> p b n", n=256)
        DEN = self.fv(WP + 6144, 4096).rearrange("p (m t) -> p m t", t=T)
        WV = self.bv(WP + 10240, 2048).rearrange("p (k n) -> p k n", n=256)
        bank = lambda b: self.ps[:, b * 512:(b + 1) * 512]
        xb = [XH]

        def nb(n):
            v = self.fv(xb[0], n)
            xb[0] += n
            return v
        qs_ = []
        for si in range(2):
            qs_.append({"raw": nb(512), "sq": nb(256).bitcast(BF16), "rs": nb(512), "pp": bank(2 * si), "p2": bank(2 * si + 1)})
        bs_ = []
        for si in range(2):
            bs_.append({"sc": [nb(512), nb(512)], "E": [nb(256).bitcast(BF16), nb(256).bitcast(BF16)],
                        "bk": [bank(4 * si + j) for j in range(4)]})
        assert xb[0] <= XH + 8192
        for g, (win, dil) in enumerate(ATT_GROUPS):
            ci_specs = []
            for which in (O_AQ, O_AK):
                for jc in range(2):
                    c0 = which + g * 256 + jc * 128
                    ci_specs.append(self.w_in[l, :, c0:c0 + 128])
            for i in range(4):
                self.wload(self.wb[i][:, 0:KC, :], ci_specs[i], KC)
            self.wload(WV, self.w_in[l, :, O_AV + g * 256:O_AV + (g + 1) * 256], KC)
            nblk = 16 // dil

            def qk_stream(wi, jc, st):
                dstT, pn, eb = ((QT, P_AQN, float(np.log(0.125))), (KT, P_AKN, 0.0))[wi]
                wbuf = self.wb[wi * 2 + jc]
                gcol = PV[:, pn + 2 * g + jc:pn + 2 * g + jc + 1]
                raw, sq, rs, ps, ps2 = st["raw"], st["sq"], st["rs"], st["pp"], st["p2"]
                for t in range(4):
                    sl = slice(t * 512, (t + 1) * 512)
                    for k in range(KC):
                        S.mm(ps, wbuf[:, k, :], self.uT[:, k, sl], start=(k == 0), stop=(k == KC - 1))
                    yield
                    S.copy("act", raw, ps)
                    S.act(sq, ps, AF.Square)
                    S.mm(ps2, self.bd64_b, sq)
                    yield
                    S.act(rs, ps2, AF.Ln, bias=EPS, scale=1.0 / 64)
                    S.act(rs, rs, AF.Exp, scale=-0.5, bias=eb)
                    n_l = 512 // dil
                    dst = dstT[:, jc, :].rearrange("p (r l) -> p r l", r=dil)[:, :, t * n_l:(t + 1) * n_l]
                    S.stt("dve", dst, raw.rearrange("p (l r) -> p r l", r=dil), gcol,
                          rs.rearrange("p (l r) -> p r l", r=dil), ALU.mult, ALU.mult)

            def v_stream():
                for b in range(16):
                    r, n = divmod(b, nblk)
                    tok = slice(n * 128 * dil + r, n * 128 * dil + r + 127 * dil + 1, dil)
                    ps = bank(4 + b % 2)[:, 0:256]
                    for k in range(KC):
                        S.mm(ps, self.uT[:, k, tok], WV[:, k, :], start=(k == 0), stop=(k == KC - 1))
                    yield
                    S.copy("act", VG[:, b, :], ps)

            interleave([qk_stream(0, 0, qs_[0]), qk_stream(0, 1, qs_[1]), v_stream()])
            interleave([qk_stream(1, 0, qs_[0]), qk_stream(1, 1, qs_[1])])
            Bc = CF[:, C_AB + g * 1024:C_AB + g * 1024 + 512]
            Bp = CF[:, C_AB + g * 1024 + 512:C_AB + g * 1024 + 1024]

            def block(b, st):
                sc, E, bk = st["sc"], st["E"], st["bk"]
                r, n = divmod(b, nblk)
                tok = slice(n * 128 * dil + r, n * 128 * dil + r + 127 * dil + 1, dil)
                qs = slice(b * 128, (b + 1) * 128)
                ks = slice((b - 1) * 128, b * 128)
                for ei, (kk, Bt) in enumerate(((qs, Bc), (ks, Bp))):
                    if ei == 1 and n == 0:
                        continue
                    pcs = [bk[2 * ei], bk[2 * ei + 1]]
                    for j in range(4):
                        jc, hh = divmod(j, 2)
                        S.mm(pcs[hh][:, jc * 128:(jc + 1) * 128], KT[64 * hh:64 * hh + 64, jc, kk],
                             QT[64 * hh:64 * hh + 64, jc, qs])
                    yield
                    for hh in range(2):
                        S.tt("dve", sc[ei].rearrange("p (m h q) -> p h m q", m=2, h=2)[:, hh],
                             pcs[hh][:, 0:256].rearrange("p (m q) -> p m q", m=2),
                             Bt.rearrange("p (m h q) -> p h m q", m=2, h=2)[:, hh], ALU.add)
                    S.act(E[ei], sc[ei], AF.Exp)
                yield
                po, pd = bk[0], bk[1]
                for j in range(4):
                    m, hh = divmod(j, 2)
                    o_out = po[64 * hh:64 * hh + 64, m * 128:(m + 1) * 128]
                    d_out = pd[64 * hh:64 * hh + 64, m * 128:(m + 1) * 128]
                    ej = slice(j * 128, (j + 1) * 128)
                    if n > 0:
                        S.mm(o_out, VG[:, b - 1, j * 64:(j + 1) * 64], E[1][:, ej], start=True, stop=False)
                    S.mm(o_out, VG[:, b, j * 64:(j + 1) * 64], E[0][:, ej], start=(n == 0), stop=True)
                    if n > 0:
                        S.mm(d_out, self.ones_b[:, 0:64], E[1][:, ej], start=True, stop=False)
                    S.mm(d_out, self.ones_b[:, 0:64], E[0][:, ej], start=(n == 0), stop=True)
                yield
                po3 = po[:, 0:256].rearrange("p (m q) -> p m q", m=2)
                pd3 = pd[:, 0:256].rearrange("p (m q) -> p m q", m=2)
                S.copy("act", Y[:, 2 * g:2 * g + 2, tok], po3)
                if g == 0:
                    S.copy("dve", DEN[:, :, tok], pd3)
                else:
                    S.tt("dve", DEN[:, :, tok], DEN[:, :, tok], pd3, ALU.add)

            for b in range(0, 16, 2):
                interleave([block(b, bs_[0]), block(b + 1, bs_[1])])
        for m in range(2):
            S.recip(DEN[:, m, :], DEN[:, m, :])
        for c in range(6):
            S.tt("dve", Y[:, c, :], Y[:, c, :], DEN[:, c % 2, :], ALU.mult)

    def branch_C(self, l, NH=2):
        S, PV, Y, CF = self.S, self.PV, self.Y, self.CF
        W, XH = A_WORK, self.XH
        ident = self.ident
        UT = CF[:, C_UT:C_UT + 128]
        self.wb = [self.bv(W + i * 512, 1024).rearrange("p (k n) -> p k n", n=128) for i in range(4)]
        wbase = [W + 2048]

        def wal(n):
            v = self.fv(wbase[0], n)
            wbase[0] += n
            return v
        slots = []
        for s_ in range(2):
            d = {}
            d["kT"], d["qT"] = wal(2048), wal(2048)
            d["Ktok"] = wal(2048).rearrange("p (n d) -> p n d", d=128)
            d["Vtok"] = wal(1024).bitcast(BF16).rearrange("p (n d) -> p n d", d=128)
            d["S2"] = [wal(128), wal(128)]
            d["sqo"] = wal(64).bitcast(BF16)
            slots.append(d)
        BGf = wal(256)
        BG = BGf.rearrange("p (n c) -> p n c", c=16)
        GAM = wal(128).rearrange("p (n c) -> p n c", c=8)
        NB = wal(128).rearrange("p (n c) -> p n c", c=8)

        def mk_cs(al):
            c = {"xin": [al(520)[:, 0:515] for _ in range(2)]}
            c["cbuf"], c["sbuf"], c["rs"] = al(512), al(512), al(512)
            c["sq"] = al(256).bitcast(BF16)
            return c
        css = [mk_cs(wal)]
        colsb = wal(32)
        assert wbase[0] <= W + 22016, wbase[0] - W
        cbuf = css[0]["cbuf"]
        der = self.fv(A_DER, 64)
        nea = der[:, 24:32]
        xbase = [XH]

        def nb(n=128):
            v = self.fv(xbase[0], n)
            xbase[0] += n
            return v
        bank = lambda b: self.ps[:, b * 512:b * 512 + 128]
        lanes = []
        for li in range(2):
            ln = {}
            for nm in ("Gbc", "tmpT", "tmp2", "Ebc", "Aneg", "Fneg", "Td", "Wm"):
                ln[nm] = nb()
            ln["DT"], ln["dec"] = ln["tmpT"], ln["tmp2"]
            for nm in ("M", "MT", "XTn"):
                ln[nm] = [nb(), nb()]
            ln["tcol"], ln["kdc"] = colsb[:, 2 * li:2 * li + 1], colsb[:, 2 * li + 1:2 * li + 2]
            ln["L"] = [bank(2 * li), bank(2 * li + 1)]
            lanes.append(ln)
        ring = {nm: [nb(), nb(), nb()] for nm in ("TT", "QKT", "KgT", "QgT", "Kdec", "Vb")}
        ring["cdc"] = [colsb[:, 8 + i:9 + i] for i in range(3)]
        recp = {nm: nb() for nm in ("Rb", "ub", "osb", "rso")}
        assert xbase[0] <= XH + 8192, xbase[0] - XH
        RB = [bank(4), bank(5)]
        HB = [self.ps[:, 6 * 512:7 * 512], self.ps[:, 7 * 512:8 * 512]]

        def ctx(d, n, lane=None):
            dd = dict(d)
            dd.update(ring)
            if lane is not None:
                dd.update(lane)
                L = lane["L"]
            else:
                dd.update(recp)
                L = [None, None]
            dd["pt"] = [L[0], L[1], L[0], L[1], L[0], L[1], L[0], RB[0], RB[1], RB[0], RB[1], RB[0]]
            return dd

        WBA = self.wb[0]
        self.wload(WBA, self.w_in[l, :, O_BETA:O_BETA + 128], KC)
        pbg = self.bank()
        for n in range(16):
            for k in range(KC):
                S.mm(pbg[:, n * 16:(n + 1) * 16], self.uT[:, k, n * 128:(n + 1) * 128], WBA[:, k, 0:16],
                     start=(k == 0), stop=(k == KC - 1))
        pbg3 = pbg[:, 0:256].rearrange("p (n c) -> p n c", c=16)
        S.act(nea, PV[:, P_ALOG:P_ALOG + 8], AF.Exp)
        S.ts("dve", nea, nea, -1.0, None, ALU.mult)
        xg = cbuf[:, 0:256]
        xg3 = xg.rearrange("p (n c) -> p n c", c=16)
        S.act(BGf, pbg[:, 0:256], AF.Sigmoid)
        S.memset("dve", xg, 1.0)
        for h in range(8):
            S.act(xg3[:, :, 8 + h], pbg3[:, :, 8 + h], AF.Exp, bias=PV[:, P_DTB + h:P_DTB + h + 1])
        S.act(xg, xg, AF.Ln, bias=1.0)
        for h in range(8):
            S.ts("dve", BG[:, :, 8 + h], xg3[:, :, 8 + h], nea[:, h:h + 1], None, ALU.mult)
        for h in range(8):
            S.ts("dve", NB[:, :, h], BG[:, :, h], -1.0, None, ALU.mult)
        pgam = self.bank()
        S.mm(pgam[:, 0:256], UT, BGf)
        pgam3 = pgam[:, 0:256].rearrange("p (n c) -> p n c", c=16)
        for h in range(8):
            S.copy("act", GAM[:, :, h], pgam3[:, :, 8 + h])

        col = lambda j, k: PV[:, P_DCW + k * 24 + j:P_DCW + k * 24 + j + 1]

        def convsilu(ps, j, t, cs):
            xin, cb = cs["xin"], cs["cbuf"]
            xi = xin[t % 2]
            if t == 0:
                S.memset("pool", xi[:, 0:3], 0.0)
            else:
                S.copy("pool", xi[:, 0:3], xin[(t - 1) % 2][:, 512:515])
            S.copy("act", xi[:, 3:515], ps)
            S.ts("dve", cb, xi[:, 3:515], col(j, 3), None, ALU.mult)
            for k in (2, 1, 0):
                if k == 1:
                    yield
                S.stt("dve", cb, xi[:, k:k + 512], col(j, k), cb, ALU.mult, ALU.add)
            yield
            S.act(cs["sbuf"], cb, AF.Silu)

        def head_prep(h, d, get, gbase, i, cs):
            kT, qT, Ktok, Vtok = d["kT"], d["qT"], d["Ktok"], d["Vtok"]
            bk = [HB[0], HB[0], HB[1], HB[1]]

            def proj_to(ps, wbuf, sl):
                for k in range(KC):
                    S.mm(ps, wbuf[:, k, :], self.uT[:, k, sl], start=(k == 0), stop=(k == KC - 1))
                    if k % 2 == 1 and k < KC - 1:
                        yield
            wz = get(gbase + i)
            for t in range(4):
                sl = slice(t * 512, (t + 1) * 512)
                yield from proj_to(bk[t % 2], wz, sl)
                yield
                S.act(Y[:, h, sl], bk[t % 2], AF.Silu)
            lnq = float(np.log(128.0 ** -0.5))
            for stage, (jofs, dst, eb, tok3) in enumerate(((0, qT, lnq, None), (8, kT, 0.0, Ktok), (16, None, None, Vtok))):
                wbuf = get(gbase + (stage + 1) + i)
                for t in range(4):
                    sl = slice(t * 512, (t + 1) * 512)
                    ps = bk[t % 2]
                    yield from proj_to(ps, wbuf, sl)
                    yield
                    yield from convsilu(ps, jofs + h, t, cs)
                    yield
                    if dst is not None:
                        S.act(cs["sq"], cs["sbuf"], AF.Square)
                        S.mm(bk[2], self.ones_b, cs["sq"])
                        yield
                        S.act(cs["rs"], bk[2], AF.Ln, bias=EPS)
                        S.act(cs["rs"], cs["rs"], AF.Exp, scale=-0.5, bias=eb)
                        S.tt("dve", dst[:, sl], cs["sbuf"], cs["rs"], ALU.mult)
                        src = dst[:, sl]
                    else:
                        src = cs["sbuf"]
                    if tok3 is not None:
                        for ii in range(4):
                            S.tr(bk[3][:, ii * 128:(ii + 1) * 128], src[:, ii * 128:(ii + 1) * 128], ident)
                        yield
                        S.copy("act", tok3[:, 4 * t:4 * t + 4, :], bk[3].rearrange("p (n d) -> p n d", d=128))

        def prep(h, n, d):
            par = n % 3
            sl = slice(n * 128, (n + 1) * 128)
            kT, qT, Ktok, Vtok = d["kT"], d["qT"], d["Ktok"], d["Vtok"]
            M, MT, XTn = d["M"], d["MT"], d["XTn"]
            gc, bc = BG[:, n, 8 + h:9 + h], BG[:, n, h:h + 1]
            nbc, gam = NB[:, n, h:h + 1], GAM[:, n, h:h + 1]
            S.ts("dve", d["Gbc"], CF[:, C_ONES:C_ONES + 128], gc, None, ALU.mult)
            pG = d["pt"][0]
            S.mm(pG, d["Gbc"], UT)
            pKK = d["pt"][1]
            S.mm(pKK, kT[:, sl], kT[:, sl])
            yield
            S.stt("dve", d["tmp2"], pG, gam, CF[:, C_NEGLT:C_NEGLT + 128], ALU.subtract, ALU.subtract)
            S.act(d["dec"], d["tmp2"], AF.Exp, scale=-1.0)
            S.stt("dve", d["tmpT"], pG, gam, CF[:, C_NEGUT:C_NEGUT + 128], ALU.subtract, ALU.add)
            S.act(d["DT"], d["tmpT"], AF.Exp)
            S.act(d["Ebc"], pG, AF.Exp)
            S.tt("dve", d["tcol"], pG[:, 127:128], gam, ALU.subtract)
            S.act(d["kdc"], d["tcol"], AF.Exp)
            S.act(d["cdc"][par], pG[:, 127:128], AF.Exp)
            pKQ = d["pt"][2]
            S.mm(pKQ, kT[:, sl], qT[:, sl])
            yield
            S.stt("dve", d["Aneg"], pKK, nbc, d["dec"], ALU.mult, ALU.mult)
            S.tt("pool", M[0], d["Aneg"], CF[:, C_BD:C_BD + 128], ALU.mult)
            S.tt("pool", d["Fneg"], d["Aneg"], CF[:, C_OFF:C_OFF + 128], ALU.mult)
            S.tt("dve", d["QKT"][par], pKQ, d["DT"], ALU.mult)
            S.tt("pool", d["KgT"][par], kT[:, sl], d["Ebc"], ALU.mult)
            S.tt("pool", d["QgT"][par], qT[:, sl], d["Ebc"], ALU.mult)
            S.ts("pool", d["Kdec"][par], Ktok[:, n, :], d["kdc"], None, ALU.mult)
            S.ts("pool", d["Vb"][par], Vtok[:, n, :], bc, None, ALU.mult)
            pT = d["pt"][3]
            S.tr(pT, M[0], ident)
            yield
            S.copy("act", MT[0], pT)
            S.tt("dve", XTn[0], MT[0], ident, ALU.add)
            for k in range(5):
                a, b = k % 2, (k + 1) % 2
                pM = d["pt"][4]
                S.mm(pM, MT[a], M[a])
                if k < 4:
                    pMT = d["pt"][5]
                    S.mm(pMT, M[a], MT[a])
                yield
                S.copy("act", M[b], pM)
                if k < 4:
                    S.copy("dve", MT[b], pMT)
                pX = d["pt"][6]
                S.mm(pX, M[b], XTn[a])
                yield
                S.tt("dve", XTn[b], XTn[a], pX, ALU.add)
            TdT = XTn[1]
            pTd = d["pt"][3]
            S.tr(pTd, TdT, ident)
            pW = d["pt"][4]
            S.mm(pW, d["Fneg"], TdT)
            yield
            S.copy("act", d["Td"], pTd)
            S.copy("dve", d["Wm"], pW)
            pP = d["pt"][5]
            S.mm(pP, d["Td"], d["Wm"])
            yield
            S.tt("dve", d["TT"][par], TdT, pP, ALU.add)

        def rec(h, n, d):
            par = n % 3
            sl = slice(n * 128, (n + 1) * 128)
            nbc = NB[:, n, h:h + 1]
            Sold, Snew = d["S2"][n % 2], d["S2"][(n + 1) % 2]
            Rb, ub, osb, rso, sqo = d["Rb"], d["ub"], d["osb"], d["rso"], d["sqo"]
            if n > 0:
                pKS = d["pt"][7]
                S.mm(pKS, d["KgT"][par], Sold)
                yield
                S.stt("dve", Rb, pKS, nbc, d["Vb"][par], ALU.mult, ALU.add)
                Rv = Rb
            else:
                Rv = d["Vb"][par]
            pu = d["pt"][8]
            S.mm(pu, d["TT"][par], Rv)
            yield
            S.copy("act", ub, pu)
            pS = d["pt"][10]
            S.mm(pS, d["Kdec"][par], ub)
            po = d["pt"][9]
            if n > 0:
                S.mm(po, Sold, d["QgT"][par], start=True, stop=False)
            S.mm(po, ub, d["QKT"][par], start=(n == 0), stop=True)
            yield
            if n > 0:
                S.ts("dve", Snew, Sold, d["cdc"][par], None, ALU.mult)
                S.tt("dve", Snew, pS, Snew, ALU.add)
            else:
                S.copy("dve", Snew, pS)
            S.act(sqo, po, AF.Square)
            S.copy("act", osb, po)
            ps2 = d["pt"][11]
            S.mm(ps2, self.ones_b, sqo)
            yield
            S.act(rso, ps2, AF.Ln, bias=EPS, scale=1.0 / 128)
            S.act(rso, rso, AF.Exp, scale=-0.5)
            S.stt("dve", osb, osb, PV[:, P_DON + h:P_DON + h + 1], rso, ALU.mult, ALU.mult)
            S.tt("dve", Y[:, h, sl], Y[:, h, sl], osb, ALU.mult)

        specs = []
        for h in range(8):
            for o in (O_DZ, O_DQ, O_DK, O_DV):
                specs.append((self.w_in[l, :, o + h * 128:o + (h + 1) * 128], KC))
        get = self.stream(specs, self.wb)

        def run_head(h, d, extra):
            active = {}
            if extra is not None:
                active["hp"] = extra
            fin_prep, fin_rec = set(), set()
            st = {"np": 0, "nr": 0, "rec_busy": False, "lane": [False, False]}
            while True:
                n = st["np"]
                if n < 16 and not st["lane"][n % 2] and (n < 3 or (n - 3) in fin_rec):
                    active[("p", n)] = prep(h, n, ctx(d, n, lanes[n % 2]))
                    st["lane"][n % 2] = True
                    st["np"] += 1
                    continue
                if not st["rec_busy"] and st["nr"] < 16 and st["nr"] in fin_prep:
                    active[("r", st["nr"])] = rec(h, st["nr"], ctx(d, st["nr"]))
                    st["rec_busy"] = True
                if not active:
                    break
                prio = lambda k_: 2 if k_ == "hp" else (0 if k_[0] == "r" else 1)
                for key in sorted(active, key=lambda k_: (prio(k_), 0 if k_ == "hp" else k_[1])):
                    try:
                        next(active[key])
                    except StopIteration:
                        del active[key]
                        if key == "hp":
                            pass
                        elif key[0] == "p":
                            fin_prep.add(key[1])
                            st["lane"][key[1] % 2] = False
                        else:
                            fin_rec.add(key[1])
                            st["rec_busy"] = False
                            st["nr"] += 1

        interleave([head_prep(0, slots[0], get, 0, 0, css[0])])
        for h in range(8):
            nxt = head_prep(h + 1, slots[(h + 1) % 2], get, 4 * (h + 1), 0, css[0]) if h + 1 < 8 else None
            run_head(h, slots[h % 2], nxt)


def host_tables():
    cf = np.zeros((128, NCF), np.float32)
    i = np.arange(128)
    cf[:, C_ID:C_ID + 128] = np.eye(128, dtype=np.float32)
    P, Fr = np.meshgrid(i, i, indexing="ij")
    cf[:, C_NEGUT:C_NEGUT + 128] = np.where(Fr >= P, 0.0, NEG)
    cf[:, C_NEGLT:C_NEGLT + 128] = np.where(Fr < P, 0.0, NEG)
    cf[:, C_BD:C_BD + 128] = ((P // 64) == (Fr // 64)).astype(np.float32)
    cf[:, C_OFF:C_OFF + 128] = ((P >= 64) & (Fr < 64)).astype(np.float32)
    cf[:, C_UT:C_UT + 128] = (P <= Fr).astype(np.float32)
    cf[:, C_ONES:C_ONES + 128] = 1.0
    h = np.arange(1, 13, dtype=np.float32)
    slopes = np.exp2(np.float32(-8.0) * h / np.float32(12)).astype(np.float32)
    for g, (win, dil) in enumerate(ATT_GROUPS):
        for j in range(4):
            sl = slopes[g * 4 + j]
            k_, q_ = P, Fr
            dcur = (q_ - k_).astype(np.float32)
            cur = np.where(k_ <= q_, -sl * (dcur * dil), NEG)
            dprev = (q_ + 128 - k_).astype(np.float32)
            prev = np.where(k_ >= q_, -sl * (dprev * dil), NEG)
            base = C_AB + g * 1024
            cf[:, base + j * 128: base + (j + 1) * 128] = cur
            cf[:, base + 512 + j * 128: base + 512 + (j + 1) * 128] = prev
    cb = np.zeros((128, 256), np.float32)
    cb[:, 0:128] = 1.0
    cb[:, 128:256] = ((P // 64) == (Fr // 64)).astype(np.float32)
    return cf, cb


def host_params(inp):
    pv = np.zeros((L, 128, NPV), np.float32)
    col = lambda v: np.ascontiguousarray(np.asarray(v, np.float32).reshape(-1, 128).T)
    for l in range(L):
        pv[l, :, P_N1:P_N1 + 8] = col(inp["ffn1_norm"][l])
        pv[l, :, P_NM:P_NM + 8] = col(inp["mix_norm"][l])
        pv[l, :, P_N2:P_N2 + 8] = col(inp["ffn2_norm"][l])
        for k in range(4):
            pv[l, :, P_RCW + k * 8:P_RCW + (k + 1) * 8] = col(inp["rg_conv_w"][l, k])
            pv[l, :, P_DCW + k * 24:P_DCW + (k + 1) * 24] = col(inp["dn_conv_w"][l, k])
        pv[l, :, P_RCB:P_RCB + 8] = col(inp["rg_conv_b"][l])
        pv[l, :, P_RBR:P_RBR + 8] = col(inp["rg_b_r"][l])
        pv[l, :, P_RBI:P_RBI + 8] = col(inp["rg_b_i"][l])
        pv[l, :, P_LAM:P_LAM + 8] = col(inp["rg_lambda"][l])
        pv[l, :, P_AQN:P_AQN + 6] = col(inp["att_q_norm"][l])
        pv[l, :, P_AKN:P_AKN + 6] = col(inp["att_k_norm"][l])
        pv[l, :, P_DON:P_DON + 8] = col(inp["dn_out_norm"][l])
        pv[l, :, P_ALOG:P_ALOG + 8] = np.broadcast_to(np.asarray(inp["dn_a_log"][l], np.float32)[None, :], (128, 8))
        pv[l, :, P_DTB:P_DTB + 8] = np.broadcast_to(np.asarray(inp["dn_dt_bias"][l], np.float32)[None, :], (128, 8))
    rgw = np.zeros((L, 128, 2, 8, 128), np.float32)
    for l in range(L):
        for gi, nm in enumerate(("rg_w_r", "rg_w_i")):
            w = np.asarray(inp[nm][l], np.float32)
            for c in range(8):
                rgw[l, 0:64, gi, c, 0:64] = w[2 * c]
                rgw[l, 64:128, gi, c, 64:128] = w[2 * c + 1]
    return pv, rgw.reshape(L, 128, 2048)


_CACHE = {}


def run(inputs, nlayers=L, stop_after=None, ncores=8, dbgs=()):
    key = (nlayers, stop_after, tuple(dbgs))
    if key not in _CACHE:
        b_ = Builder(nlayers, stop_after)
        b_.dbgs = set(dbgs)
        _CACHE[key] = (b_.build(), b_)
    nc, b_ = _CACHE[key]
    cf, cb = host_tables()
    pv, rgw = host_params(inputs)
    x = np.asarray(inputs["x"], np.float32)
    shared = {"cf": cf, "cb": cb, "pv": pv, "rgw": rgw}
    for f in ("ffn1", "ffn2"):
        for s in ("w_gate", "w_up", "w_down"):
            shared[f + "_" + s] = np.ascontiguousarray(np.asarray(inputs[f + "_" + s], np.float32))
    for nm in ("w_in", "w_branch", "w_out"):
        shared[nm] = np.ascontiguousarray(np.asarray(inputs[nm], np.float32))
    in_maps = []
    for b in range(ncores):
        m = dict(shared)
        m["xT"] = np.ascontiguousarray(x[b].T)
        in_maps.append(m)
    res = run_bass_kernel_spmd(nc, in_maps, core_ids=list(range(ncores)))
    if b_.dbg_names:
        run.dbg = {n: res.results[0][n] for n in b_.dbg_names}
    return np.stack([np.ascontiguousarray(r["y"].T) for r in res.results], axis=0)


def kernel(**inputs):
    return run(inputs).astype(np.float32)
```
